# Optimizing a Trainium2 kernel written in Bass

```python
import math
import jax, jax.numpy as jnp
from jax import lax
import numpy as np

D_MODEL = 1024
BATCH = 8
SEQ = 8192
DEPTH = 4

HEAD_DIM = 64
N_Q_HEADS = 16
N_KV_HEADS = 4
GROUP = N_Q_HEADS // N_KV_HEADS
WINDOW = 128
BLOCK = 128
ROT_DIM = HEAD_DIM // 4
ROPE_THETA = 500000.0
Q_WIDTH = N_Q_HEADS * HEAD_DIM
KV_WIDTH = N_KV_HEADS * HEAD_DIM
D_RNN = 1024
RNN_BLOCKS = 16
RNN_BW = D_RNN // RNN_BLOCKS
CONV_WIDTH = 4
LRU_C = 8.0
POOL_WINDOWS = (2, 4, 8, 16)
N_POOL_GROUPS = len(POOL_WINDOWS)
D_POOL = 1024
POOL_GW = D_POOL // N_POOL_GROUPS
N_BRANCHES = 3
D_FF = 4 * D_MODEL
EPS = 1e-6
NEG_INF = -1e30

D_IN = Q_WIDTH + 2 * KV_WIDTH + 2 * D_RNN + D_POOL + N_BRANCHES * D_MODEL
SPLITS = tuple(np.cumsum([Q_WIDTH, KV_WIDTH, KV_WIDTH, D_RNN, D_RNN, D_POOL]).tolist())

kernel_name = "hybrid_swa_rglru_pool_gated_block"


def rms_norm(x, g):
    xf = x.astype(jnp.float32)
    y = xf * lax.rsqrt(jnp.mean(xf * xf, axis=-1, keepdims=True) + EPS)
    return (y * g.astype(jnp.float32)).astype(x.dtype)


def rope_tables(seq):
    inv_freq = ROPE_THETA ** (-jnp.arange(0, ROT_DIM, 2, dtype=jnp.float32) / ROT_DIM)
    ang = jnp.arange(seq, dtype=jnp.float32)[:, None] * inv_freq[None, :]
    return jnp.cos(ang)[:, None, :], jnp.sin(ang)[:, None, :]


def apply_partial_rope(x, cos, sin):
    half = ROT_DIM // 2
    xf = x.astype(jnp.float32)
    x1, x2, rest = xf[..., :half], xf[..., half:ROT_DIM], xf[..., ROT_DIM:]
    out = jnp.concatenate([x1 * cos - x2 * sin, x2 * cos + x1 * sin, rest], axis=-1)
    return out.astype(x.dtype)


def sliding_window_attention(q, k, v, sinks):
    b, s = q.shape[0], q.shape[1]
    nb = s // BLOCK
    qb = q.reshape(b, nb, BLOCK, N_KV_HEADS, GROUP, HEAD_DIM)
    kb = k.reshape(b, nb, BLOCK, N_KV_HEADS, HEAD_DIM)
    vb = v.reshape(b, nb, BLOCK, N_KV_HEADS, HEAD_DIM)
    pad = ((0, 0), (1, 0), (0, 0), (0, 0), (0, 0))
    kcat = jnp.concatenate([jnp.pad(kb[:, :-1], pad), kb], axis=2)
    vcat = jnp.concatenate([jnp.pad(vb[:, :-1], pad), vb], axis=2)
    scores = jnp.einsum('bnqhgd,bnkhd->bnhgqk', qb, kcat).astype(jnp.float32) * (HEAD_DIM ** -0.5)
    qi = jnp.arange(BLOCK)[:, None]
    ki = jnp.arange(2 * BLOCK)[None, :]
    rel = qi + BLOCK - ki
    band = (rel >= 0) & (rel < WINDOW)
    has_prev = (jnp.arange(nb) > 0)[:, None, None]
    valid = band[None] & (has_prev | (ki >= BLOCK)[None])
    scores = jnp.where(valid[None, :, None, None], scores, NEG_INF)
    sink = sinks.astype(jnp.float32).reshape(N_KV_HEADS, GROUP)[None, None, :, :, None, None]
    m = jnp.maximum(jnp.max(scores, axis=-1, keepdims=True), sink)
    p = jnp.exp(scores - m)
    p = p / (jnp.sum(p, axis=-1, keepdims=True) + jnp.exp(sink - m))
    o = jnp.einsum('bnhgqk,bnkhd->bnqhgd', p.astype(v.dtype), vcat)
    return o.reshape(b, s, Q_WIDTH)


def causal_depthwise_conv(x, w, bias):
    s = x.shape[1]
    xp = jnp.pad(x, ((0, 0), (CONV_WIDTH - 1, 0), (0, 0)))
    y = bias
    for tap in range(CONV_WIDTH):
        y = y + xp[:, tap:tap + s] * w[tap]
    return y


def _lru_combine(e1, e2):
    a1, b1 = e1
    a2, b2 = e2
    return a1 * a2, a2 * b1 + b2


def rg_lru(x, w_a, b_a, w_i, b_i, lam):
    b, s, c = x.shape
    xf = x.astype(jnp.float32)
    xb = xf.reshape(b, s, RNN_BLOCKS, RNN_BW)
    r = jax.nn.sigmoid(jnp.einsum('bshi,hij->bshj', xb, w_a.astype(jnp.float32)).reshape(b, s, c) + b_a)
    i = jax.nn.sigmoid(jnp.einsum('bshi,hij->bshj', xb, w_i.astype(jnp.float32)).reshape(b, s, c) + b_i)
    log_a = LRU_C * r * jax.nn.log_sigmoid(lam.astype(jnp.float32))
    a = jnp.exp(log_a)
    u = jnp.sqrt(-jnp.expm1(2.0 * log_a)) * (i * xf)
    _, h = lax.associative_scan(_lru_combine, (a, u), axis=1)
    return h


def multi_scale_pool(p, w_groups, scale):
    b, s, c = p.shape
    pf = p.astype(jnp.float32)
    cs = jnp.cumsum(pf, axis=1)
    t = jnp.arange(s)
    outs = []
    for g, w in enumerate(POOL_WINDOWS):
        sl = slice(g * POOL_GW, (g + 1) * POOL_GW)
        c_g = cs[..., sl]
        lag = jnp.pad(c_g[:, :s - w], ((0, 0), (w, 0), (0, 0)))
        cnt = jnp.minimum(t + 1, w).astype(jnp.float32)[None, :, None]
        outs.append((c_g - lag) / cnt - pf[..., sl])
    pooled = jnp.stack(outs, axis=2)
    mixed = jnp.einsum('bsgi,gij->bsgj', pooled, w_groups.astype(jnp.float32)).reshape(b, s, c)
    return (mixed * scale.astype(jnp.float32)).astype(p.dtype)


def setup_inputs(seed: int = 0) -> dict:
    key = jax.random.key(seed)
    ks = jax.random.split(key, 24)
    f32 = jnp.float32

    def nrm(k, shape, fan_in):
        return jax.random.normal(k, shape, f32) * (fan_in ** -0.5)

    def gain(k, shape):
        return 1.0 + 0.05 * jax.random.normal(k, shape, f32)

    L = DEPTH
    a0 = jax.random.uniform(ks[11], (L, D_RNN), f32, 0.9, 0.999)
    return {
        "x": jax.random.normal(ks[0], (BATCH, SEQ, D_MODEL), f32),
        "norm_mix_pre": gain(ks[1], (L, D_MODEL)),
        "norm_mix_post": gain(ks[2], (L, D_MODEL)),
        "w_in": nrm(ks[3], (L, D_MODEL, D_IN), D_MODEL),
        "attn_sinks": 0.5 * jax.random.normal(ks[4], (L, N_Q_HEADS), f32),
        "w_attn_br": nrm(ks[5], (L, Q_WIDTH, D_MODEL), Q_WIDTH),
        "conv_w": nrm(ks[6], (L, CONV_WIDTH, D_RNN), CONV_WIDTH),
        "conv_b": 0.01 * jax.random.normal(ks[7], (L, D_RNN), f32),
        "w_rg_a": nrm(ks[8], (L, RNN_BLOCKS, RNN_BW, RNN_BW), RNN_BW),
        "b_rg_a": 0.01 * jax.random.normal(ks[9], (L, D_RNN), f32),
        "w_rg_i": nrm(ks[10], (L, RNN_BLOCKS, RNN_BW, RNN_BW), RNN_BW),
        "b_rg_i": 0.01 * jax.random.normal(ks[12], (L, D_RNN), f32),
        "lru_lambda": jnp.log(a0) - jnp.log1p(-a0),
        "w_rnn_br": nrm(ks[13], (L, D_RNN, D_MODEL), D_RNN),
        "w_pool_groups": nrm(ks[14], (L, N_POOL_GROUPS, POOL_GW, POOL_GW), POOL_GW),
        "pool_scale": gain(ks[15], (L, D_POOL)),
        "w_pool_br": nrm(ks[16], (L, D_POOL, D_MODEL), D_POOL),
        "w_out": nrm(ks[17], (L, D_MODEL, D_MODEL), D_MODEL),
        "norm_mlp_pre": gain(ks[18], (L, D_MODEL)),
        "norm_mlp_post": gain(ks[19], (L, D_MODEL)),
        "w_mlp_up": nrm(ks[20], (L, D_MODEL, D_FF), D_MODEL),
        "w_mlp_down": nrm(ks[21], (L, D_FF, D_MODEL), D_FF),
    }


def reference(x, norm_mix_pre, norm_mix_post, w_in, attn_sinks, w_attn_br, conv_w, conv_b,
              w_rg_a, b_rg_a, w_rg_i, b_rg_i, lru_lambda, w_rnn_br, w_pool_groups, pool_scale,
              w_pool_br, w_out, norm_mlp_pre, norm_mlp_post, w_mlp_up, w_mlp_down):
    b, s, _ = x.shape
    cos, sin = rope_tables(s)
    h = x
    for l in range(DEPTH):
        u = rms_norm(h, norm_mix_pre[l])
        proj = u @ w_in[l]
        q, k, v, xr, yr, pp, gt = jnp.split(proj, SPLITS, axis=-1)
        q = apply_partial_rope(q.reshape(b, s, N_Q_HEADS, HEAD_DIM), cos, sin)
        k = apply_partial_rope(k.reshape(b, s, N_KV_HEADS, HEAD_DIM), cos, sin)
        v = v.reshape(b, s, N_KV_HEADS, HEAD_DIM)
        attn_br = sliding_window_attention(q, k, v, attn_sinks[l]) @ w_attn_br[l]
        xc = causal_depthwise_conv(xr, conv_w[l], conv_b[l])
        hr = rg_lru(xc, w_rg_a[l], b_rg_a[l], w_rg_i[l], b_rg_i[l], lru_lambda[l])
        rnn_br = (hr * jax.nn.gelu(yr.astype(jnp.float32))).astype(h.dtype) @ w_rnn_br[l]
        pool_br = multi_scale_pool(pp, w_pool_groups[l], pool_scale[l]) @ w_pool_br[l]
        g = jax.nn.sigmoid(gt.astype(jnp.float32)).reshape(b, s, N_BRANCHES, D_MODEL)
        merged = (g[:, :, 0] * attn_br + g[:, :, 1] * rnn_br + g[:, :, 2] * pool_br).astype(h.dtype)
        mix = merged @ w_out[l]
        h = h + rms_norm(mix, norm_mix_post[l])
        m = rms_norm(h, norm_mlp_pre[l])
        y = jnp.square(jax.nn.relu(m @ w_mlp_up[l])) @ w_mlp_down[l]
        h = h + rms_norm(y, norm_mlp_post[l])
    return h
```

```python
import numpy as np
from contextlib import ExitStack
import concourse.bass as bass
import concourse.mybir as mybir
from concourse.bass_utils import run_bass_kernel_spmd

F32 = mybir.dt.float32
BF16 = mybir.dt.bfloat16
AF = mybir.ActivationFunctionType
ALU = mybir.AluOpType

D = 1024
SEQ = 8192
NCORE = 8
DEPTH = 4
TT = 512
NB = TT // 128
D_IN = 7680
D_FF = 4096
EPS = 1e-6
NV = 13
NEG = -30000.0
WSLOT_ELEMS = 4096
NWSLOT = 3
import os as _os
SAME_ENGINE_SYNC = _os.environ.get('KSES', '1') == '1'
FUSED = True

C_Q, C_K, C_V, C_XR, C_YR, C_PP, C_GT = 0, 1024, 1280, 1536, 2560, 3584, 4608
POOL_W = (2, 4, 8, 16)


class Res:
    __slots__ = ("name", "lw", "rd")

    def __init__(self, name):
        self.name = name
        self.lw = None
        self.rd = {}


class Op:
    __slots__ = ("eng", "fn", "deps", "sig", "tok", "chan", "isdma", "lab")

    def __init__(self, eng, fn, deps, chan, isdma):
        self.eng = eng
        self.fn = fn
        self.deps = deps
        self.sig = False
        self.tok = None
        self.chan = chan
        self.isdma = isdma


class Sched:
    def __init__(self, nc):
        self.nc = nc
        self.ops = []
        self.eng = {"pe": nc.tensor, "act": nc.scalar, "dve": nc.vector, "pool": nc.gpsimd, "sp": nc.sync}

    def op(self, eng, meth, *args, reads=(), writes=(), dma=False, chan=None, **kw):
        idx = len(self.ops)
        deps = set()
        for r in reads:
            if r.lw is not None:
                deps.add(r.lw)
        for w in writes:
            if w.lw is not None:
                deps.add(w.lw)
            deps.update(w.rd.values())
        deps.discard(idx)
        if dma:
            fn = meth
        else:
            fn = (meth, args, kw)
        self.ops.append(Op(eng, fn, deps, chan, dma))
        self.ops[-1].lab = getattr(self, 'lab', '')
        key = ("dma", idx) if dma else eng
        for r in reads:
            r.rd[key] = idx
        for w in writes:
            w.lw = idx
            w.rd = {}
        return idx

    def emit(self, es):
        nc = self.nc
        ops = self.ops
        for i, o in enumerate(ops):
            keep = set()
            for d in o.deps:
                od = ops[d]
                if not od.isdma and not o.isdma and od.eng == o.eng:
                    if o.eng == "pe" or not SAME_ENGINE_SYNC:
                        continue
                keep.add(d)
            o.deps = keep
            for d in keep:
                ops[d].sig = True
        sems = {}

        def getsem(key):
            if key not in sems:
                sems[key] = es.enter_context(nc.semaphore("s_" + str(key)))
            return sems[key]

        count = {}
        for o in ops:
            if o.isdma:
                continue
            if o.sig:
                count[o.eng] = count.get(o.eng, 0) + 1
                o.tok = (o.eng, count[o.eng])
        waited = {e: {} for e in self.eng}
        chcount = {}
        nwait = 0
        import os
        lim = int(os.environ.get('KLIMIT', '0')) or len(ops)
        for oi, o in enumerate(ops):
            if oi >= lim:
                break
            e = self.eng[o.eng]
            need = {}
            for d in o.deps:
                k, v = ops[d].tok
                if need.get(k, 0) < v:
                    need[k] = v
            for k, v in need.items():
                if waited[o.eng].get(k, 0) < v:
                    e.wait_ge(getsem(k), v)
                    waited[o.eng][k] = v
                    nwait += 1
            if o.isdma:
                ins = o.fn()
                if not isinstance(ins, (list, tuple)):
                    ins = [ins]
                s = getsem(o.chan)
                for i_ in ins:
                    i_.then_inc(s, 16)
                chcount[o.chan] = chcount.get(o.chan, 0) + 16 * len(ins)
                o.tok = (o.chan, chcount[o.chan])
            else:
                meth, args, kw = o.fn
                ins = meth(*args, **kw)
                if o.sig:
                    ins.then_inc(getsem(o.eng), 1)
        for ch, v in chcount.items():
            nc.sync.wait_ge(getsem(ch), v)
        return nwait, len(ops), len(sems)


def _const_tables():
    ident = np.eye(128, dtype=np.float32)
    k = np.arange(128)[:, None]
    q = np.arange(128)[None, :]
    mcur = np.where(k <= q, 0.0, NEG).astype(np.float32)
    mprev = np.where(k > q, 0.0, NEG).astype(np.float32)
    masks = np.concatenate([np.tile(mprev, (1, 4)), np.tile(mcur, (1, 4))], axis=1)
    mats = []
    tp = np.arange(128)[:, None]
    t = np.arange(128)[None, :]
    for w in POOL_W:
        cur = np.where((tp <= t) & (tp > t - w), 1.0 / w, 0.0) - (tp == t)
        prev = np.where(tp - 128 > t - w, 1.0 / w, 0.0)
        cnt = np.minimum(t + 1, w).astype(np.float64)
        first = np.where((tp <= t) & (tp > t - w), 1.0 / cnt, 0.0) - (tp == t)
        mats += [cur, prev, first]
    mats = np.concatenate([m.astype(np.float32) for m in mats], axis=1)
    cbf = np.concatenate([ident, masks, mats], axis=1).astype(np.float32)
    inv_freq = (500000.0 ** (-(np.arange(0, 16, 2, dtype=np.float32)) / np.float32(16))).astype(np.float32)
    ang = (np.arange(SEQ, dtype=np.float32)[:, None] * inv_freq[None, :]).astype(np.float32)
    c = np.cos(ang).astype(np.float32)
    s = np.sin(ang).astype(np.float32)
    rope = np.concatenate([c, c, -s, s], axis=1).astype(np.float32)
    return ident, cbf, rope


def build(L, NT):
    nc = bass.Bass("TRN2", target_bir_lowering=False)
    S = Sched(nc)
    ntok = NT * TT

    def din(name, shape):
        return nc.dram_tensor(name, list(shape), F32, kind="ExternalInput").ap()

    x_d = din("x", [ntok, D])
    w_in_d = din("w_in", [L, D, D_IN])
    w_attn_d = din("w_attn_br", [L, D, D])
    w_rnn_d = din("w_rnn_br", [L, D, D])
    w_pool_d = din("w_pool_br", [L, D, D])
    w_out_d = din("w_out", [L, D, D])
    w_up_d = din("w_mlp_up", [L, D, D_FF])
    w_down_d = din("w_mlp_down", [L, D_FF, D])
    w_rga_d = din("w_rg_a", [L, 16, 64, 64])
    w_rgi_d = din("w_rg_i", [L, 16, 64, 64])
    w_pg_d = din("w_pool_groups", [L, 4, 256, 256])
    vecs_d = din("vecs", [L, 128, NV * 8])
    sink_d = din("sinks", [128, L * 8])
    identf_d = din("identf", [128, 128])
    cbf_d = din("cbf", [128, 2688])
    rope_d = din("rope", [ntok, 32])
    out_d = nc.dram_tensor("out", [ntok, D], F32, kind="ExternalOutput").ap()

    def dscr(name, shape):
        return nc.dram_tensor(name, list(shape), BF16, kind="Internal").ap()

    w_in_b = dscr("w_in_b", [L, D, D_IN])
    w_attn_b = dscr("w_attn_b", [L, D, D])
    w_rnn_b = dscr("w_rnn_b", [L, D, D])
    w_pool_b = dscr("w_pool_b", [L, D, D])
    w_out_b = dscr("w_out_b", [L, D, D])
    w_up_b = dscr("w_up_b", [L, D, D_FF])
    w_down_b = dscr("w_down_b", [L, D_FF, D])
    w_rga_b = dscr("w_rga_b", [L, 16, 64, 64])
    w_rgi_b = dscr("w_rgi_b", [L, 16, 64, 64])
    w_pg_b = dscr("w_pg_b", [L, 4, 256, 256])
    wg_b = dscr("wg_b", [L, 8, 128, 3072])
    wbr_b = dscr("wbr_b", [L, 8, 128, 3072])

    es = ExitStack()

    def sb(name, shape, dt):
        return es.enter_context(nc.sbuf_tensor("sb_" + name, list(shape), dt))

    with es:
        wslot = [sb("wslot%d" % i, [128, WSLOT_ELEMS], BF16) for i in range(NWSLOT)]
        R_wslot = [Res("wslot%d" % i) for i in range(NWSLOT)]
        xst = sb("xst", [128, NB, D], F32)
        R_xs = [Res("xs%d" % i) for i in range(8)]
        mixsb = xst[:].rearrange("p b (c t) -> p (b c) t", t=512)
        h = sb("h", [128, 8, TT], F32)
        R_h = [Res("h%d" % c) for c in range(8)]
        uT = sb("uT", [128, 8, TT], BF16)
        R_uT = [Res("uT%d" % c) for c in range(8)]
        qtm = sb("qtm", [128, NB, D], BF16)
        R_qtm = Res("qtm")
        pooledT = qtm[:].rearrange("p b (c t) -> p (b c) t", t=512)
        merged = pooledT
        ktm = sb("ktm", [128, NB, 256], BF16)
        R_ktm = [Res("ktm%d" % b) for b in range(NB)]
        vtm = sb("vtm", [128, NB + 1, 256], BF16)
        R_vtm = [Res("vtm%d" % b) for b in range(NB + 1)]
        ptm = sb("ptm", [128, NB + 1, D], BF16)
        R_ptm = [Res("ptm%d" % b) for b in range(NB + 1)]
        qT = sb("qT", [128, 8, TT], BF16)
        R_qT = [[Res("qT%d_%d" % (g, b)) for b in range(NB)] for g in range(4)]
        klo = sb("klo", [128, 4, 128 * (NB + 1)], BF16)
        khi = sb("khi", [128, 4, 128 * (NB + 1)], BF16)
        R_kTd = [Res("kTd%d" % b) for b in range(NB + 1)]
        pT = sb("pT", [128, 2, 2, 512], BF16)
        R_pT = [[Res("pT%d_%d" % (i, j)) for j in range(2)] for i in range(2)]
        rnnT = sb("rnnT", [128, 8, TT], BF16)
        R_rnnT = Res("rnnT")
        mixedT = sb("mixedT", [128, 8, TT], BF16)
        R_mixedT = Res("mixedT")
        th = sb("th", [128, 3, TT], F32)
        R_th = [Res("th%d" % i) for i in range(3)]
        rstd = sb("rstd", [128, TT], F32)
        R_rstd = Res("rstd")
        sq = sb("sq", [128, 2, TT], BF16)
        R_sq = [Res("sq0"), Res("sq1")]
        xrb = sb("xrb", [128, 2, TT + 3], F32)
        R_xrb = [Res("xrb0"), Res("xrb1")]
        yb = sb("yb", [128, 2, TT], F32)
        R_yb = [Res("yb0"), Res("yb1")]
        xc = sb("xc", [128, TT], F32); R_xc = Res("xc")
        xcb = sb("xcb", [128, TT], BF16); R_xcb = Res("xcb")
        tha = sb("tha", [128, TT], F32); R_tha = Res("tha")
        thi = sb("thi", [128, TT], F32); R_thi = Res("thi")
        aa = sb("aa", [128, TT], F32); R_aa = Res("aa")
        ss_ = sb("ss_", [128, TT], F32); R_ss = Res("ss_")
        x1 = sb("x1", [128, TT], F32); R_x1 = Res("x1")
        rl = sb("rl", [128, 2, TT], F32); R_rl = [Res("rl0"), Res("rl1")]
        ropet = sb("ropet", [128, NB, 32], F32); R_ropet = Res("ropet")
        rt1 = sb("rt1", [128, 2, 8, 16], F32); R_rt1 = [Res("rt1_0"), Res("rt1_1")]
        rt2 = sb("rt2", [128, 2, 8, 16], F32); R_rt2 = [Res("rt2_0"), Res("rt2_1")]
        att = sb("att", [128, 2, 256], F32); R_att = [Res("att0"), Res("att1")]
        identf = sb("identf", [128, 128], F32); R_const = Res("const")
        cbf = sb("cbf", [128, 2688], BF16)
        ones = sb("ones", [128, 128], BF16)
        cpow = sb("cpow", [128, 5], F32)
        vec = sb("vec", [128, L, NV, 8], F32)
        der = sb("der", [128, L, 4, 8], F32)
        esk = sb("esk", [128, L, 8], F32)
        wsm = sb("wsm", [128, 2, 4096], BF16)
        R_wsm = [Res("wsm0"), Res("wsm1")]
        kst = sb("kst", [128, L, 4, 128], BF16); R_kst = [Res("kst%d" % l) for l in range(L)]
        vst = sb("vst", [128, L, 256], BF16); R_vst = [Res("vst%d" % l) for l in range(L)]
        pst = sb("pst", [128, L, D], BF16); R_pst = [Res("pst%d" % l) for l in range(L)]
        ctail = sb("ctail", [128, L, 8, 3], F32); R_ctail = [Res("ctail%d" % l) for l in range(L)]
        hst = sb("hst", [128, L, 8], F32); R_hst = [Res("hst%d" % l) for l in range(L)]

        ident_bf = cbf[:, 0:128]
        mask_prev = cbf[:, 128:640]
        mask_cur = cbf[:, 640:1152]

        def poolmat(wi, kind):
            o = 1152 + (wi * 3 + kind) * 128
            return cbf[:, o:o + 128]

        banks = [es.enter_context(nc.psum_tensor("bank%d" % i, [128, 512], F32)) for i in range(8)]
        R_bank = [Res("bank%d" % i) for i in range(8)]
        bstate = {"i": 0}

        def nbank():
            i = bstate["i"]
            bstate["i"] = (i + 1) % 7
            return i
        SSB = 7

        R_scr = [Res("scr%d" % l) for l in range(L)]
        S.op("sp", lambda: [nc.sync.dma_start(out=identf[:], in_=identf_d),
                            nc.sync.dma_start(out=vec[:].rearrange("p l n c -> p l (n c)"), in_=vecs_d.rearrange("l p n -> p l n")),
                            nc.sync.dma_start(out=esk[:].rearrange("p l c -> p (l c)"), in_=sink_d)],
             writes=[R_const], dma=True, chan="c0")
        S.op("pool", lambda: [nc.gpsimd.dma_start(out=cbf[:], in_=cbf_d)], writes=[R_const], dma=True, chan="c1")
        for l in range(L):
            def castfn(l=l):
                ins = []
                for (dst, src, rows) in [(w_in_b, w_in_d, D), (w_out_b, w_out_d, D), (w_up_b, w_up_d, D),
                                         (w_down_b, w_down_d, D_FF)]:
                    for r0 in range(0, rows, 256):
                        ins.append(nc.gpsimd.dma_start(out=dst[l, r0:r0 + 256, :], in_=src[l, r0:r0 + 256, :]))
                for j in range(8):
                    for b_ in range(3):
                        dstg = wg_b[l, j].rearrange("p (k b c) -> p k b c", k=8, b=3)[:, :, b_, :]
                        srcg = w_in_d[l][:, C_GT + b_ * 1024 + j * 128:C_GT + b_ * 1024 + (j + 1) * 128].rearrange("(k p) c -> p k c", p=128)
                        ins.append(nc.gpsimd.dma_start(out=dstg, in_=srcg))
                        dstb = wbr_b[l, j].rearrange("p (k b c) -> p k b c", k=8, b=3)[:, :, b_, :]
                        srcb = (w_attn_d, w_rnn_d, w_pool_d)[b_][l][:, j * 128:(j + 1) * 128].rearrange("(k p) c -> p k c", p=128)
                        ins.append(nc.gpsimd.dma_start(out=dstb, in_=srcb))
                ins.append(nc.gpsimd.dma_start(out=w_rga_b[l].rearrange("a b c -> (a b) c"), in_=w_rga_d[l].rearrange("a b c -> (a b) c")))
                ins.append(nc.gpsimd.dma_start(out=w_rgi_b[l].rearrange("a b c -> (a b) c"), in_=w_rgi_d[l].rearrange("a b c -> (a b) c")))
                ins.append(nc.gpsimd.dma_start(out=w_pg_b[l].rearrange("a b c -> (a b) c"), in_=w_pg_d[l].rearrange("a b c -> (a b) c")))
                return ins
            S.op("pool", castfn, writes=[R_scr[l]], dma=True, chan="cast%d" % l)

        S.op("dve", nc.vector.memset, ones[:], 1.0, writes=[R_const])
        S.op("dve", nc.vector.memset, cpow[:, 0:1], -0.5, writes=[R_const])
        S.op("dve", nc.vector.memset, cpow[:, 1:2], 0.5, writes=[R_const])
        S.op("dve", nc.vector.memset, cpow[:, 2:3], EPS, writes=[R_const])
        S.op("dve", nc.vector.memset, cpow[:, 3:4], 0.7978845608, writes=[R_const])
        S.op("dve", nc.vector.memset, cpow[:, 4:5], 1.0, writes=[R_const])
        S.op("dve", nc.vector.memset, wsm[:], 0.0, writes=R_wsm)
        S.op("dve", nc.vector.memset, ctail[:], 0.0, writes=R_ctail)
        S.op("dve", nc.vector.memset, hst[:], 0.0, writes=R_hst)
        S.op("dve", nc.vector.memset, kst[:], 0.0, writes=R_kst)
        S.op("dve", nc.vector.memset, klo[:], 0.0, writes=R_kTd)
        S.op("dve", nc.vector.memset, khi[:], 0.0, writes=R_kTd)
        S.op("dve", nc.vector.memset, vst[:], 0.0, writes=R_vst)
        S.op("dve", nc.vector.memset, pst[:], 0.0, writes=R_pst)
        for l in range(L):
            S.op("dve", nc.vector.tensor_scalar, out=der[:, l, 0, :], in0=vec[:, l, 9, :], scalar1=0.5, scalar2=None, op0=ALU.mult, reads=[R_const], writes=[R_const])
            S.op("dve", nc.vector.tensor_scalar, out=der[:, l, 1, :], in0=vec[:, l, 10, :], scalar1=0.5, scalar2=None, op0=ALU.mult, reads=[R_const], writes=[R_const])
            S.op("act", nc.scalar.activation, out=der[:, l, 2, :], in_=vec[:, l, 11, :], func=AF.Exp, scale=-1.0, reads=[R_const], writes=[R_const])
            S.op("act", nc.scalar.activation, out=der[:, l, 2, :], in_=der[:, l, 2, :], func=AF.Ln, bias=1.0, scale=1.0, reads=[R_const], writes=[R_const])
            S.op("dve", nc.vector.tensor_scalar, out=der[:, l, 3, :], in0=der[:, l, 2, :], scalar1=-4.0, scalar2=None, op0=ALU.mult, reads=[R_const], writes=[R_const])
            S.op("dve", nc.vector.tensor_scalar, out=der[:, l, 2, :], in0=der[:, l, 2, :], scalar1=-8.0, scalar2=None, op0=ALU.mult, reads=[R_const], writes=[R_const])
        S.op("act", nc.scalar.activation, out=esk[:], in_=esk[:], func=AF.Exp, reads=[R_const], writes=[R_const])

        wstate = {"n": 0}

        def load_unit(pieces, l):
            n = wstate["n"]
            wstate["n"] = n + 1
            si = n % NWSLOT
            views = []
            off = 0
            dmas = []
            for src in pieces:
                rows, cols = src.shape
                kc = rows // 128
                dst = wslot[si][:, off:off + kc * cols].rearrange("p (k c) -> p k c", c=cols)
                views.append(dst)
                dmas.append((dst, src.rearrange("(k p) c -> p k c", p=128)))
                off += kc * cols
            assert off <= WSLOT_ELEMS
            S.op("sp", lambda: [nc.sync.dma_start(out=d_, in_=s_) for d_, s_ in dmas],
                 reads=[R_scr[l]], writes=[R_wslot[si]], dma=True, chan="w%d" % si)
            return views, R_wslot[si]

        def load_unit_raw(src, l):
            n = wstate["n"]
            wstate["n"] = n + 1
            si = n % NWSLOT
            ncol = src.shape[1]
            assert ncol <= WSLOT_ELEMS
            dst = wslot[si][:, 0:ncol]
            S.op("sp", lambda: [nc.sync.dma_start(out=dst, in_=src)], reads=[R_scr[l]], writes=[R_wslot[si]], dma=True, chan="w%d" % si)
            return dst, R_wslot[si]

        def prenorm(l, vi):
            for c in range(8):
                if c % 2 == 0:
                    S.op("act", nc.scalar.activation, out=sq[:, c % 2, :], in_=h[:, c, :], func=AF.Square,
                         reads=[R_h[c]], writes=[R_sq[c % 2]])
                else:
                    S.op("dve", nc.vector.tensor_tensor, out=sq[:, c % 2, :], in0=h[:, c, :], in1=h[:, c, :], op=ALU.mult,
                         reads=[R_h[c]], writes=[R_sq[c % 2]])
                S.op("pe", nc.tensor.matmul, banks[SSB][:], ones[:], sq[:, c % 2, :], start=(c == 0), stop=(c == 7),
                     reads=[R_sq[c % 2], R_const], writes=[R_bank[SSB]])
            S.op("act", nc.scalar.activation, out=rstd[:], in_=banks[SSB][:], func=AF.Ln, scale=1.0 / D, bias=cpow[:, 2:3],
                 reads=[R_bank[SSB], R_const], writes=[R_rstd])
            S.op("act", nc.scalar.activation, out=rstd[:], in_=rstd[:], func=AF.Exp, scale=-0.5,
                 reads=[R_rstd], writes=[R_rstd])
            for c in range(8):
                S.op("dve", nc.vector.scalar_tensor_tensor, out=uT[:, c, :], in0=h[:, c, :], scalar=vec[:, l, vi, c:c + 1], in1=rstd[:], op0=ALU.mult, op1=ALU.mult,
                     reads=[R_h[c], R_rstd, R_const], writes=[R_uT[c]])

        def postnorm(l, vi, srcfn, scale):
            for j in range(8):
                bi = srcfn(j)
                S.op("act", nc.scalar.activation, out=sq[:, j % 2, :], in_=banks[bi][:], func=AF.Square, scale=scale,
                     reads=[R_bank[bi]], writes=[R_sq[j % 2]])
                S.op("act", nc.scalar.activation, out=mixsb[:, j, :], in_=banks[bi][:], func=AF.Copy, scale=scale,
                     reads=[R_bank[bi]], writes=[R_xs[j]])
                S.op("pe", nc.tensor.matmul, banks[SSB][:], ones[:], sq[:, j % 2, :], start=(j == 0), stop=(j == 7),
                     reads=[R_sq[j % 2], R_const], writes=[R_bank[SSB]])
            S.op("act", nc.scalar.activation, out=rstd[:], in_=banks[SSB][:], func=AF.Ln, scale=1.0 / D, bias=cpow[:, 2:3],
                 reads=[R_bank[SSB], R_const], writes=[R_rstd])
            S.op("act", nc.scalar.activation, out=rstd[:], in_=rstd[:], func=AF.Exp, scale=-0.5,
                 reads=[R_rstd], writes=[R_rstd])
            for j in range(8):
                S.op("dve", nc.vector.scalar_tensor_tensor, out=mixsb[:, j, :], in0=mixsb[:, j, :], scalar=vec[:, l, vi, j:j + 1], in1=rstd[:], op0=ALU.mult, op1=ALU.mult,
                     reads=[R_xs[j], R_rstd, R_const], writes=[R_xs[j]])
                S.op("pool", nc.gpsimd.tensor_tensor, out=h[:, j, :], in0=h[:, j, :], in1=mixsb[:, j, :], op=ALU.add,
                     reads=[R_xs[j], R_h[j]], writes=[R_h[j]])

        def fm_group(wview, Rw, col0, rhs_t, R_rhs, nk=8, k0=0, bank=None, start=True, stop=True):
            bi = nbank() if bank is None else bank
            for k in range(nk):
                S.op("pe", nc.tensor.matmul, banks[bi][:], wview[:, k, col0:col0 + 128], rhs_t[:, k0 + k, :],
                                                        start=(start and k == 0), stop=(stop and k == nk - 1),
                     reads=[Rw] + (R_rhs if isinstance(R_rhs, list) else [R_rhs]), writes=[R_bank[bi]])
            return bi

        def layer(t, l):
            gblock0 = (t == 0)
            win = w_in_b[l]
            ws = (t * L + l) % 2
            wv = wsm[:, ws, :]
            wrg = wv[:, 0:2048].rearrange("p (g c q) -> p g c q", g=2, c=8)
            wpg = wv[:, 2048:4096].rearrange("p (g k j) -> p g k j", g=4, k=2)

            def smallfn():
                ins = []
                for gi, src in enumerate((w_rga_b, w_rgi_b)):
                    s4 = src[l].rearrange("(c two) i j -> two i c j", two=2)
                    ins.append(nc.sync.dma_start(out=wrg[0:64, gi, :, 0:64], in_=s4[0]))
                    ins.append(nc.sync.dma_start(out=wrg[64:128, gi, :, 64:128], in_=s4[1]))
                ins.append(nc.sync.dma_start(out=wpg, in_=w_pg_b[l].rearrange("g (k p) j -> p g k j", p=128)))
                return ins
            S.op("sp", smallfn, reads=[R_scr[l]], writes=[R_wsm[ws]], dma=True, chan="wsm%d" % ws)

            S.op("pool", nc.gpsimd.tensor_copy, out=klo[0:64, :, 0:128], in_=kst[0:64, l, :, :], reads=[R_kst[l]], writes=[R_kTd[0]])
            S.op("pool", nc.gpsimd.tensor_copy, out=khi[64:128, :, 0:128], in_=kst[64:128, l, :, :], reads=[R_kst[l]], writes=[R_kTd[0]])
            S.op("pool", nc.gpsimd.tensor_copy, out=vtm[:, 0, :], in_=vst[:, l, :], reads=[R_vst[l]], writes=[R_vtm[0]])
            S.op("pool", nc.gpsimd.tensor_copy, out=ptm[:, 0, :], in_=pst[:, l, :], reads=[R_pst[l]], writes=[R_ptm[0]])

            S.lab = 'prenorm'
            prenorm(l, 0)

            S.lab = 'qkv'
            def tm_group(col0, evac):
                (wvw,), Rw = load_unit([win[:, col0:col0 + 512]], l)
                for b in range(NB):
                    bi = nbank()
                    for k in range(8):
                        S.op("pe", nc.tensor.matmul, banks[bi][:], uT[:, k, b * 128:(b + 1) * 128], wvw[:, k, :], start=(k == 0), stop=(k == 7),
                             reads=[Rw, R_uT[k]], writes=[R_bank[bi]])
                    evac(b, bi)

            def rope_evac(bank3, nh, dst3, b, Rdst, par):
                cc = ropet[:, b, 0:16].unsqueeze(1).to_broadcast([128, nh, 16])
                sneg = ropet[:, b, 16:24].unsqueeze(1).to_broadcast([128, nh, 8])
                spos = ropet[:, b, 24:32].unsqueeze(1).to_broadcast([128, nh, 8])
                t1 = rt1[:, par, 0:nh, :]
                t2 = rt2[:, par, 0:nh, :]
                bi_res = bank3[1]
                bk = bank3[0]
                S.op("dve", nc.vector.tensor_tensor, out=t1, in0=bk[:, :, 0:16], in1=cc, op=ALU.mult, reads=[bi_res, R_ropet], writes=[R_rt1[par]])
                S.op("dve", nc.vector.tensor_tensor, out=t2[:, :, 0:8], in0=bk[:, :, 8:16], in1=sneg, op=ALU.mult, reads=[bi_res, R_ropet], writes=[R_rt2[par]])
                S.op("dve", nc.vector.tensor_tensor, out=t2[:, :, 8:16], in0=bk[:, :, 0:8], in1=spos, op=ALU.mult, reads=[bi_res, R_ropet], writes=[R_rt2[par]])
                S.op("pool", nc.gpsimd.tensor_tensor, out=dst3[:, :, 0:16], in0=t1, in1=t2, op=ALU.add, reads=[R_rt1[par], R_rt2[par]], writes=[Rdst])
                S.op("act", nc.scalar.copy, out=dst3[:, :, 16:64], in_=bk[:, :, 16:64], reads=[bi_res], writes=[Rdst])

            def evac_q(half):
                def f(b, bi):
                    bk = banks[bi][:].rearrange("p (h d) -> p h d", d=64)
                    dst = qtm[:, b, half * 512:(half + 1) * 512].rearrange("p (h d) -> p h d", d=64)
                    rope_evac((bk, R_bank[bi]), 8, dst, b, R_qtm, half)
                return f

            def evac_kv(b, bi):
                bk = banks[bi][:, 0:256].rearrange("p (h d) -> p h d", d=64)
                dst = ktm[:, b, :].rearrange("p (h d) -> p h d", d=64)
                rope_evac((bk, R_bank[bi]), 4, dst, b, R_ktm[b], 0)
                S.op("act", nc.scalar.copy, out=vtm[:, b + 1, :], in_=banks[bi][:, 256:512], reads=[R_bank[bi]], writes=[R_vtm[b + 1]])

            tm_group(C_Q, evac_q(0))
            tm_group(C_Q + 512, evac_q(1))
            tm_group(C_K, evac_kv)

            S.lab = 'qkT'
            for b in range(NB):
                bi = nbank()
                bkb = banks[bi][:].bitcast(BF16)
                for c in range(8):
                    S.op("pe", nc.tensor.transpose, bkb[:, c * 128:(c + 1) * 128], qtm[:, b, c * 128:(c + 1) * 128], ident_bf,
                         reads=[R_qtm, R_const], writes=[R_bank[bi]])
                S.op("act", nc.scalar.copy, out=qT[:, :, b * 128:(b + 1) * 128], in_=bkb.rearrange("p (c t) -> p c t", t=128),
                     reads=[R_bank[bi]], writes=[R_qT[g][b] for g in range(4)])
                bi2 = nbank()
                bk2 = banks[bi2][:].bitcast(BF16)
                for g in range(4):
                    S.op("pe", nc.tensor.transpose, bk2[0:64, g * 128:(g + 1) * 128], ktm[:, b, g * 64:(g + 1) * 64], ident_bf,
                         reads=[R_ktm[b], R_const], writes=[R_bank[bi2]])
                    S.op("pe", nc.tensor.transpose, bk2[64:128, g * 128:(g + 1) * 128], ktm[:, b, g * 64:(g + 1) * 64], ident_bf,
                         reads=[R_ktm[b], R_const], writes=[R_bank[bi2]])
                S.op("dve", nc.vector.tensor_copy, out=klo[0:64, :, (b + 1) * 128:(b + 2) * 128], in_=bk2[0:64, 0:512].rearrange("p (g t) -> p g t", t=128),
                     reads=[R_bank[bi2]], writes=[R_kTd[b + 1]])
                S.op("act", nc.scalar.copy, out=khi[64:128, :, (b + 1) * 128:(b + 2) * 128], in_=bk2[64:128, 0:512].rearrange("p (g t) -> p g t", t=128),
                     reads=[R_bank[bi2]], writes=[R_kTd[b + 1]])

            S.lab = 'lru'
            LT = [
                dict(xc=xc[:], xcb=xcb[:], tha=tha[:], thi=thi[:], aa=aa[:], ss=ss_[:], x1=x1[:],
                     Rxc=[R_xc], Rxcb=[R_xcb], Rtha=[R_tha], Rthi=[R_thi], Raa=[R_aa], Rss=[R_ss], Rx1=[R_x1]),
                dict(xc=th[:, 0, :], xcb=sq[:, 1, :], tha=th[:, 1, :], thi=th[:, 2, :], aa=rl[:, 0, :], ss=rl[:, 1, :], x1=att[:].rearrange("p a b -> p (a b)"),
                     Rxc=[R_th[0]], Rxcb=[R_sq[1]], Rtha=[R_th[1]], Rthi=[R_th[2]], Raa=[R_rl[0]], Rss=[R_rl[1]], Rx1=[R_att[0], R_att[1]]),
            ]

            def lru_front(c, par, bx):
                T_ = LT[par]
                S.op("pool", nc.gpsimd.tensor_copy, out=xrb[:, par, 0:3], in_=ctail[:, l, c, :], reads=[R_ctail[l]], writes=[R_xrb[par]])
                S.op("act", nc.scalar.copy, out=xrb[:, par, 3:TT + 3], in_=banks[bx][:], reads=[R_bank[bx]], writes=[R_xrb[par]])
                S.op("pool", nc.gpsimd.tensor_copy, out=ctail[:, l, c, :], in_=xrb[:, par, TT:TT + 3], reads=[R_xrb[par]], writes=[R_ctail[l]])
                S.op("act", nc.scalar.activation, out=T_["xc"], in_=xrb[:, par, 0:TT], func=AF.Identity, scale=vec[:, l, 4, c:c + 1], bias=vec[:, l, 8, c:c + 1],
                     reads=[R_xrb[par], R_const], writes=T_["Rxc"])
                for tap in range(1, 4):
                    S.op("dve", nc.vector.scalar_tensor_tensor, out=T_["xc"], in0=xrb[:, par, tap:tap + TT], scalar=vec[:, l, 4 + tap, c:c + 1], in1=T_["xc"], op0=ALU.mult, op1=ALU.add,
                         reads=[R_xrb[par], R_const] + T_["Rxc"], writes=T_["Rxc"])
                S.op("pool", nc.gpsimd.tensor_copy, out=T_["xcb"], in_=T_["xc"], reads=T_["Rxc"], writes=T_["Rxcb"])

            def lru_y(c, par, by):
                S.op("act", nc.scalar.copy, out=yb[:, par, :], in_=banks[by][:], reads=[R_bank[by]], writes=[R_yb[par]])

            def lru_back(c, par):
                T_ = LT[par]
                ba = nbank()
                S.op("pe", nc.tensor.matmul, banks[ba][:], wrg[:, 0, c, :], T_["xcb"], start=True, stop=True, reads=[R_wsm[ws]] + T_["Rxcb"], writes=[R_bank[ba]])
                bi_ = nbank()
                S.op("pe", nc.tensor.matmul, banks[bi_][:], wrg[:, 1, c, :], T_["xcb"], start=True, stop=True, reads=[R_wsm[ws]] + T_["Rxcb"], writes=[R_bank[bi_]])
                S.op("act", nc.scalar.activation, out=T_["tha"], in_=banks[ba][:], func=AF.Tanh, bias=der[:, l, 0, c:c + 1], scale=0.5, reads=[R_bank[ba], R_const], writes=T_["Rtha"])
                S.op("act", nc.scalar.activation, out=T_["thi"], in_=banks[bi_][:], func=AF.Tanh, bias=der[:, l, 1, c:c + 1], scale=0.5, reads=[R_bank[bi_], R_const], writes=T_["Rthi"])
                S.op("act", nc.scalar.activation, out=T_["x1"], in_=yb[:, par, :], func=AF.Square, reads=[R_yb[par]], writes=T_["Rx1"])
                S.op("act", nc.scalar.activation, out=T_["x1"], in_=T_["x1"], func=AF.Identity, scale=0.0356774081, bias=cpow[:, 3:4], reads=T_["Rx1"] + [R_const], writes=T_["Rx1"])
                S.op("pool", nc.gpsimd.tensor_tensor, out=T_["x1"], in0=T_["x1"], in1=yb[:, par, :], op=ALU.mult, reads=T_["Rx1"] + [R_yb[par]], writes=T_["Rx1"])
                S.op("act", nc.scalar.activation, out=T_["aa"], in_=T_["tha"], func=AF.Exp, bias=der[:, l, 3, c:c + 1], scale=der[:, l, 3, c:c + 1], reads=T_["Rtha"] + [R_const], writes=T_["Raa"])
                S.op("act", nc.scalar.activation, out=T_["ss"], in_=T_["tha"], func=AF.Exp, bias=der[:, l, 2, c:c + 1], scale=der[:, l, 2, c:c + 1], reads=T_["Rtha"] + [R_const], writes=T_["Rss"])
                S.op("act", nc.scalar.activation, out=T_["x1"], in_=T_["x1"], func=AF.Tanh, reads=T_["Rx1"], writes=T_["Rx1"])
                S.op("act", nc.scalar.activation, out=T_["ss"], in_=T_["ss"], func=AF.Sqrt, scale=-1.0, bias=cpow[:, 4:5], reads=T_["Rss"] + [R_const], writes=T_["Rss"])
                S.op("dve", nc.vector.scalar_tensor_tensor, out=T_["thi"], in0=T_["thi"], scalar=1.0, in1=T_["xc"], op0=ALU.add, op1=ALU.mult, reads=T_["Rthi"] + T_["Rxc"], writes=T_["Rthi"])
                S.op("dve", nc.vector.tensor_tensor, out=T_["ss"], in0=T_["ss"], in1=T_["thi"], op=ALU.mult, reads=T_["Rss"] + T_["Rthi"], writes=T_["Rss"])
                S.op("dve", nc.vector.tensor_tensor_scan, out=T_["tha"], data0=T_["aa"], data1=T_["ss"], initial=hst[:, l, c:c + 1], op0=ALU.mult, op1=ALU.add,
                     reads=T_["Raa"] + T_["Rss"] + [R_hst[l]] + T_["Rtha"], writes=T_["Rtha"])
                S.op("pool", nc.gpsimd.tensor_copy, out=hst[:, l, c:c + 1], in_=T_["tha"][:, TT - 1:TT], reads=T_["Rtha"], writes=[R_hst[l]])
                S.op("dve", nc.vector.scalar_tensor_tensor, out=T_["x1"], in0=T_["x1"], scalar=1.0, in1=yb[:, par, :], op0=ALU.add, op1=ALU.mult, reads=T_["Rx1"] + [R_yb[par]], writes=T_["Rx1"])
                S.op("dve", nc.vector.scalar_tensor_tensor, out=rnnT[:, c, :], in0=T_["tha"], scalar=0.25, in1=T_["x1"], op0=ALU.mult, op1=ALU.mult, reads=T_["Rtha"] + T_["Rx1"], writes=[R_rnnT])

            prev_c = None
            for i in range(4):
                views, Rw = load_unit([win[:, C_XR + 256 * i:C_XR + 256 * (i + 1)], win[:, C_YR + 256 * i:C_YR + 256 * (i + 1)]], l)
                for cc_ in range(2):
                    c = 2 * i + cc_
                    par = c % 2
                    bx = fm_group(views[0], Rw, cc_ * 128, uT, R_uT)
                    lru_front(c, par, bx)
                    by = fm_group(views[1], Rw, cc_ * 128, uT, R_uT)
                    lru_y(c, par, by)
                    if prev_c is not None:
                        lru_back(prev_c, prev_c % 2)
                    prev_c = c
            lru_back(prev_c, prev_c % 2)

            S.lab = 'pp'
            def evac_p(half):
                def f(b, bi):
                    eng = "dve" if (b % 2 == 0) else "act"
                    if eng == "dve":
                        S.op("dve", nc.vector.tensor_copy, out=ptm[:, b + 1, half * 512:(half + 1) * 512], in_=banks[bi][:], reads=[R_bank[bi]], writes=[R_ptm[b + 1]])
                    else:
                        S.op("act", nc.scalar.copy, out=ptm[:, b + 1, half * 512:(half + 1) * 512], in_=banks[bi][:], reads=[R_bank[bi]], writes=[R_ptm[b + 1]])
                return f
            tm_group(C_PP, evac_p(0))
            tm_group(C_PP + 512, evac_p(1))

            S.lab = 'attn'
            def att_A(g, b):
                first = gblock0 and b == 0
                pb = (g * NB + b) % 2
                kinds = []
                for kind in ((1,) if first else (0, 1)):
                    bi = nbank()
                    kinds.append(kind)
                    kb = b + kind
                    msk = mask_cur if kind == 1 else mask_prev
                    S.lab = 'mask'
                    S.op("pe", nc.tensor.matmul, banks[bi][:], ident_bf, msk, start=True, stop=False, reads=[R_const], writes=[R_bank[bi]])
                    S.lab = 'attn'
                    S.op("pe", nc.tensor.matmul, banks[bi][:, 0:256], klo[:, g, kb * 128:(kb + 1) * 128], qT[:, 2 * g:2 * g + 2, b * 128:(b + 1) * 128], start=False, stop=False,
                         reads=[R_kTd[kb], R_qT[g][b]], writes=[R_bank[bi]])
                    S.op("pe", nc.tensor.matmul, banks[bi][:, 256:512], khi[:, g, kb * 128:(kb + 1) * 128], qT[:, 2 * g:2 * g + 2, b * 128:(b + 1) * 128], start=False, stop=True,
                         reads=[R_kTd[kb], R_qT[g][b]], writes=[R_bank[bi]])
                    S.op("act", nc.scalar.activation, out=pT[:, pb, kind, :], in_=banks[bi][:], func=AF.Exp, scale=0.125, reads=[R_bank[bi]], writes=[R_pT[pb][kind]])
                return kinds

            def att_B(g, b, kinds):
                pb = (g * NB + b) % 2
                nd = nbank()
                for half in range(2):
                    ps = slice(0, 64) if half == 0 else slice(64, 128)
                    cs = slice(0, 256) if half == 0 else slice(256, 512)
                    for which in range(2):
                        ocs = slice(0, 256) if which == 0 else slice(256, 512)
                        for ii, kind in enumerate(kinds):
                            kb = b + kind
                            if which == 0:
                                S.op("pe", nc.tensor.matmul, banks[nd][ps, ocs], vtm[:, kb, g * 64:(g + 1) * 64], pT[:, pb, kind, cs], start=(ii == 0), stop=(ii == len(kinds) - 1),
                                     reads=[R_vtm[kb], R_pT[pb][kind]], writes=[R_bank[nd]])
                            else:
                                S.op("pe", nc.tensor.matmul, banks[nd][ps, ocs], ones[:, 0:64], pT[:, pb, kind, cs], start=(ii == 0), stop=(ii == len(kinds) - 1),
                                     reads=[R_const, R_pT[pb][kind]], writes=[R_bank[nd]])
                ek = esk[:, l, 2 * g:2 * g + 2].unsqueeze(2).to_broadcast([128, 2, 128])
                S.op("dve", nc.vector.tensor_tensor, out=att[:, pb, :].rearrange("p (c t) -> p c t", t=128), in0=banks[nd][:, 256:512].rearrange("p (c t) -> p c t", t=128), in1=ek, op=ALU.add,
                     reads=[R_bank[nd], R_const], writes=[R_att[pb]])
                S.op("dve", nc.vector.reciprocal, out=att[:, pb, :], in_=att[:, pb, :], reads=[R_att[pb]], writes=[R_att[pb]])
                S.op("dve", nc.vector.tensor_tensor, out=qT[:, 2 * g:2 * g + 2, b * 128:(b + 1) * 128], in0=banks[nd][:, 0:256].rearrange("p (c t) -> p c t", t=128), in1=att[:, pb, :].rearrange("p (c t) -> p c t", t=128), op=ALU.mult,
                     reads=[R_bank[nd], R_att[pb]], writes=[R_qT[g][b]])

            units = [(g, b) for g in range(4) for b in range(NB)]
            pend = None
            for (g, b) in units:
                kinds = att_A(g, b)
                if pend is not None:
                    att_B(*pend)
                pend = (g, b, kinds)
            att_B(*pend)
            S.op("pool", nc.gpsimd.tensor_copy, out=kst[0:64, l, :, :], in_=klo[0:64, :, NB * 128:(NB + 1) * 128], reads=[R_kTd[NB]], writes=[R_kst[l]])
            S.op("pool", nc.gpsimd.tensor_copy, out=kst[64:128, l, :, :], in_=khi[64:128, :, NB * 128:(NB + 1) * 128], reads=[R_kTd[NB]], writes=[R_kst[l]])
            S.op("pool", nc.gpsimd.tensor_copy, out=vst[:, l, :], in_=vtm[:, NB, :], reads=[R_vtm[NB]], writes=[R_vst[l]])

            S.lab = 'pool'
            for c in range(8):
                wi = c // 2
                bi = nbank()
                for b in range(NB):
                    first = gblock0 and b == 0
                    S.op("pe", nc.tensor.matmul, banks[bi][:, b * 128:(b + 1) * 128], ptm[:, b + 1, c * 128:(c + 1) * 128], poolmat(wi, 2 if first else 0), start=True, stop=first,
                         reads=[R_ptm[b + 1], R_const], writes=[R_bank[bi]])
                    if not first:
                        S.op("pe", nc.tensor.matmul, banks[bi][:, b * 128:(b + 1) * 128], ptm[:, b, c * 128:(c + 1) * 128], poolmat(wi, 1), start=False, stop=True,
                             reads=[R_ptm[b], R_const], writes=[R_bank[bi]])
                S.op("act", nc.scalar.copy, out=pooledT[:, c, :], in_=banks[bi][:], reads=[R_bank[bi]], writes=[R_qtm])
            S.op("pool", nc.gpsimd.tensor_copy, out=pst[:, l, :], in_=ptm[:, NB, :], reads=[R_ptm[NB]], writes=[R_pst[l]])
            for j in range(8):
                gi, jj = j // 2, j % 2
                bi = nbank()
                for k in range(2):
                    S.op("pe", nc.tensor.matmul, banks[bi][:], wpg[:, gi, k, jj * 128:(jj + 1) * 128], pooledT[:, 2 * gi + k, :], start=(k == 0), stop=(k == 1),
                         reads=[R_wsm[ws], R_qtm], writes=[R_bank[bi]])
                S.op("dve", nc.vector.tensor_scalar, out=mixedT[:, j, :], in0=banks[bi][:], scalar1=vec[:, l, 12, j:j + 1], scalar2=None, op0=ALU.mult,
                     reads=[R_bank[bi], R_const], writes=[R_mixedT])

            S.lab = 'merge'
            R_qT_all = [R_qT[g][b] for g in range(4) for b in range(NB)]
            for j in range(8):
                gsl, Rg = load_unit_raw(wg_b[l, j], l)
                gv = [gsl.rearrange("p (k b c) -> p k b c", k=8, b=3)[:, :, bi_, :] for bi_ in range(3)]
                gb = []
                for bi_ in range(3):
                    bk = fm_group(gv[bi_], Rg, 0, uT, R_uT)
                    gb.append(bk)
                    S.op("act", nc.scalar.activation, out=th[:, bi_, :], in_=banks[bk][:], func=AF.Tanh, scale=0.5, reads=[R_bank[bk]], writes=[R_th[bi_]])
                bsl, Rb = load_unit_raw(wbr_b[l, j], l)
                bv = [bsl.rearrange("p (k b c) -> p k b c", k=8, b=3)[:, :, bi_, :] for bi_ in range(3)]
                for bi_, (src, Rsrc) in enumerate(((qT, R_qT_all), (rnnT, R_rnnT), (mixedT, R_mixedT))):
                    bk = fm_group(bv[bi_], Rb, 0, src, Rsrc)
                    S.op("dve", nc.vector.scalar_tensor_tensor, out=th[:, bi_, :], in0=th[:, bi_, :], scalar=1.0, in1=banks[bk][:], op0=ALU.add, op1=ALU.mult,
                         reads=[R_th[bi_], R_bank[bk]], writes=[R_th[bi_]])
                S.op("pool", nc.gpsimd.tensor_tensor, out=th[:, 0, :], in0=th[:, 0, :], in1=th[:, 1, :], op=ALU.add, reads=[R_th[0], R_th[1]], writes=[R_th[0]])
                S.op("pool", nc.gpsimd.tensor_tensor, out=merged[:, j, :], in0=th[:, 0, :], in1=th[:, 2, :], op=ALU.add, reads=[R_th[0], R_th[2]], writes=[R_qtm])

            S.lab = 'wout'
            wo = {}

            def wout_src(j):
                if j % 4 == 0:
                    wo["v"], wo["R"] = load_unit([w_out_b[l][:, (j // 4) * 512:(j // 4 + 1) * 512]], l)
                return fm_group(wo["v"][0], wo["R"], (j % 4) * 128, merged, R_qtm)
            postnorm(l, 1, wout_src, 0.5)

            S.lab = 'mlp_pre'
            prenorm(l, 2)
            actT = [qT[:, i, :] for i in range(8)] + [rnnT[:, i, :] for i in range(8)] + [mixedT[:, i, :] for i in range(8)] + [pooledT[:, i, :] for i in range(8)]
            R_act = [R_qT_all] * 8 + [[R_rnnT]] * 8 + [[R_mixedT]] * 8 + [[R_qtm]] * 8
            S.lab = 'up'
            for f in range(32):
                if f % 4 == 0:
                    (uv,), Ru = load_unit([w_up_b[l][:, (f // 4) * 512:(f // 4 + 1) * 512]], l)
                bk = fm_group(uv, Ru, (f % 4) * 128, uT, R_uT)
                rp = f % 2
                S.op("act", nc.scalar.activation, out=rl[:, rp, :], in_=banks[bk][:], func=AF.Relu, reads=[R_bank[bk]], writes=[R_rl[rp]])
                if f % 2 == 0:
                    S.op("dve", nc.vector.tensor_tensor, out=actT[f], in0=rl[:, rp, :], in1=rl[:, rp, :], op=ALU.mult, reads=[R_rl[rp]], writes=R_act[f])
                else:
                    S.op("pool", nc.gpsimd.tensor_tensor, out=actT[f], in0=rl[:, rp, :], in1=rl[:, rp, :], op=ALU.mult, reads=[R_rl[rp]], writes=R_act[f])
            S.lab = 'down'
            dn = {}

            def down_src(j):
                jp, jj = j // 2, j % 2
                if jj == 0:
                    dn["b"] = [nbank(), nbank()]
                    for half in range(2):
                        (dv,), Rd = load_unit([w_down_b[l][half * 2048:(half + 1) * 2048, jp * 256:(jp + 1) * 256]], l)
                        for k in range(16):
                            for j2 in range(2):
                                kk = half * 16 + k
                                S.op("pe", nc.tensor.matmul, banks[dn["b"][j2]][:], dv[:, k, j2 * 128:(j2 + 1) * 128], actT[kk], start=(kk == 0), stop=(kk == 31),
                                     reads=[Rd] + R_act[kk], writes=[R_bank[dn["b"][j2]]])
                return dn["b"][jj]
            postnorm_down(l, down_src)

        def postnorm_down(l, down_src):
            postnorm(l, 3, down_src, 1.0)

        for t in range(NT):
            S.lab = 'xin'
            S.op("sp", lambda t=t: [nc.sync.dma_start(out=xst[:], in_=x_d[t * TT:(t + 1) * TT, :].rearrange("(b p) d -> p b d", p=128)),
                                    nc.sync.dma_start(out=ropet[:], in_=rope_d[t * TT:(t + 1) * TT, :].rearrange("(b p) n -> p b n", p=128))],
                 writes=R_xs + [R_ropet], dma=True, chan="xin")
            for c in range(8):
                bi = nbank()
                for b in range(NB):
                    S.op("pe", nc.tensor.transpose, banks[bi][:, b * 128:(b + 1) * 128], xst[:, b, c * 128:(c + 1) * 128], identf[:],
                         reads=[R_xs[b * 2 + c // 4], R_const], writes=[R_bank[bi]])
                if c % 2 == 0:
                    S.op("dve", nc.vector.tensor_copy, out=h[:, c, :], in_=banks[bi][:], reads=[R_bank[bi]], writes=[R_h[c]])
                else:
                    S.op("act", nc.scalar.copy, out=h[:, c, :], in_=banks[bi][:], reads=[R_bank[bi]], writes=[R_h[c]])
            for l in range(L):
                layer(t, l)
            S.lab = 'xout'
            for b in range(NB):
                for half in range(2):
                    bi = nbank()
                    for cc_ in range(4):
                        c = half * 4 + cc_
                        S.op("pe", nc.tensor.transpose, banks[bi][:, cc_ * 128:(cc_ + 1) * 128], h[:, c, b * 128:(b + 1) * 128], identf[:],
                             reads=[R_h[c], R_const], writes=[R_bank[bi]])
                    if half == 0:
                        S.op("dve", nc.vector.tensor_copy, out=xst[:, b, half * 512:(half + 1) * 512], in_=banks[bi][:], reads=[R_bank[bi]], writes=[R_xs[b * 2 + half]])
                    else:
                        S.op("act", nc.scalar.copy, out=xst[:, b, half * 512:(half + 1) * 512], in_=banks[bi][:], reads=[R_bank[bi]], writes=[R_xs[b * 2 + half]])
            S.op("sp", lambda t=t: [nc.sync.dma_start(out=out_d[t * TT:(t + 1) * TT, :].rearrange("(b p) d -> p b d", p=128), in_=xst[:])],
                 reads=R_xs, dma=True, chan="xout")
        fin = Res("fin")
        S.op("sp", nc.sync.nop, reads=R_xs, writes=[fin] + R_xs)
        import os
        if os.environ.get('KLABELS'):
            open(os.environ['KLABELS'], 'w').write('\n'.join(o.lab for o in S.ops if o.eng == 'pe' and not o.isdma and o.lab != 'mask'))
        stats = S.emit(es)
    return nc, stats


_VEC_ORDER = ["norm_mix_pre", "norm_mix_post", "norm_mlp_pre", "norm_mlp_post", None, None, None, None,
              "conv_b", "b_rg_a", "b_rg_i", "lru_lambda", "pool_scale"]


def _pack_layer_inputs(inp, layers):
    L = len(layers)
    vecs = np.zeros((L, 128, NV, 8), np.float32)
    for li, l in enumerate(layers):
        for vi, name in enumerate(_VEC_ORDER):
            if name is None:
                v = np.asarray(inp["conv_w"])[l, vi - 4]
            else:
                v = np.asarray(inp[name])[l]
            vecs[li, :, vi, :] = v.reshape(8, 128).T
    sinks = np.zeros((128, L, 8), np.float32)
    for li, l in enumerate(layers):
        s = np.asarray(inp["attn_sinks"])[l]
        sinks[0:64, li, :] = s[0::2][None, :]
        sinks[64:128, li, :] = s[1::2][None, :]
    m = {"vecs": vecs.reshape(L, 128, NV * 8), "sinks": sinks.reshape(128, L * 8)}
    for name in ["w_in", "w_attn_br", "w_rnn_br", "w_pool_br", "w_out", "w_mlp_up", "w_mlp_down", "w_rg_a", "w_rg_i", "w_pool_groups"]:
        m[name] = np.ascontiguousarray(np.asarray(inp[name])[layers], dtype=np.float32)
    return m


_CACHE = {}


def _get_prog(L, NT):
    key = (L, NT)
    if key not in _CACHE:
        _CACHE[key] = build(L, NT)[0]
    return _CACHE[key]


def run_layers(x_all, inp, layers, ncores=NCORE, NT=SEQ // TT):
    identf, cbf, rope = _const_tables()
    nc = _get_prog(len(layers), NT)
    base = _pack_layer_inputs(inp, layers)
    base["identf"] = identf
    base["cbf"] = cbf
    base["rope"] = np.ascontiguousarray(rope[:NT * TT])
    in_maps = []
    for c in range(ncores):
        m = dict(base)
        m["x"] = np.ascontiguousarray(x_all[c], dtype=np.float32)
        in_maps.append(m)
    res = run_bass_kernel_spmd(nc, in_maps, core_ids=list(range(ncores)))
    return np.stack([np.asarray(r["out"]) for r in res.results], axis=0)


def kernel(**inputs):
    x = np.asarray(inputs["x"], dtype=np.float32)
    if FUSED:
        out = run_layers(x, inputs, list(range(DEPTH)))
    else:
        out = x
        for l in range(DEPTH):
            out = run_layers(out, inputs, [l])
    return out.astype(np.float32)
```

```python
import numpy as np
from contextlib import ExitStack
import concourse.bass as bass
import concourse.mybir as mybir
from concourse.bass_utils import run_bass_kernel_spmd

F32 = mybir.dt.float32
BF16 = mybir.dt.bfloat16
AF = mybir.ActivationFunctionType
ALU = mybir.AluOpType

D = 1024
SEQ = 8192
NCORE = 8
DEPTH = 4
TT = 512
NB = TT // 128
D_IN = 7680
D_FF = 4096
EPS = 1e-6
NV = 13
NEG = -30000.0
WSLOT_ELEMS = 4096
NWSLOT = 3
import os as _os
SAME_ENGINE_SYNC = _os.environ.get('KSES', '1') == '1'
FUSED = True

C_Q, C_K, C_V, C_XR, C_YR, C_PP, C_GT = 0, 1024, 1280, 1536, 2560, 3584, 4608
POOL_W = (2, 4, 8, 16)


class Res:
    __slots__ = ("name", "lw", "rd")

    def __init__(self, name):
        self.name = name
        self.lw = None
        self.rd = {}


class Op:
    __slots__ = ("eng", "fn", "deps", "sig", "tok", "chan", "isdma", "lab")

    def __init__(self, eng, fn, deps, chan, isdma):
        self.eng = eng
        self.fn = fn
        self.deps = deps
        self.sig = False
        self.tok = None
        self.chan = chan
        self.isdma = isdma


class Sched:
    def __init__(self, nc):
        self.nc = nc
        self.ops = []
        self.eng = {"pe": nc.tensor, "act": nc.scalar, "dve": nc.vector, "pool": nc.gpsimd, "sp": nc.sync}

    def op(self, eng, meth, *args, reads=(), writes=(), dma=False, chan=None, **kw):
        idx = len(self.ops)
        deps = set()
        for r in reads:
            if r.lw is not None:
                deps.add(r.lw)
        for w in writes:
            if w.lw is not None:
                deps.add(w.lw)
            deps.update(w.rd.values())
        deps.discard(idx)
        if dma:
            fn = meth
        else:
            fn = (meth, args, kw)
        self.ops.append(Op(eng, fn, deps, chan, dma))
        self.ops[-1].lab = getattr(self, 'lab', '')
        key = ("dma", idx) if dma else eng
        for r in reads:
            r.rd[key] = idx
        for w in writes:
            w.lw = idx
            w.rd = {}
        return idx

    def emit(self, es):
        nc = self.nc
        ops = self.ops
        for i, o in enumerate(ops):
            keep = set()
            for d in o.deps:
                od = ops[d]
                if not od.isdma and not o.isdma and od.eng == o.eng:
                    if o.eng == "pe" or not SAME_ENGINE_SYNC:
                        continue
                keep.add(d)
            o.deps = keep
            for d in keep:
                ops[d].sig = True
        sems = {}

        def getsem(key):
            if key not in sems:
                sems[key] = es.enter_context(nc.semaphore("s_" + str(key)))
            return sems[key]

        count = {}
        for o in ops:
            if o.isdma:
                continue
            if o.sig:
                count[o.eng] = count.get(o.eng, 0) + 1
                o.tok = (o.eng, count[o.eng])
        waited = {e: {} for e in self.eng}
        chcount = {}
        nwait = 0
        import os
        lim = int(os.environ.get('KLIMIT', '0')) or len(ops)
        for oi, o in enumerate(ops):
            if oi >= lim:
                break
            e = self.eng[o.eng]
            need = {}
            for d in o.deps:
                k, v = ops[d].tok
                if need.get(k, 0) < v:
                    need[k] = v
            for k, v in need.items():
                if waited[o.eng].get(k, 0) < v:
                    e.wait_ge(getsem(k), v)
                    waited[o.eng][k] = v
                    nwait += 1
            if o.isdma:
                ins = o.fn()
                if not isinstance(ins, (list, tuple)):
                    ins = [ins]
                s = getsem(o.chan)
                for i_ in ins:
                    i_.then_inc(s, 16)
                chcount[o.chan] = chcount.get(o.chan, 0) + 16 * len(ins)
                o.tok = (o.chan, chcount[o.chan])
            else:
                meth, args, kw = o.fn
                ins = meth(*args, **kw)
                if o.sig:
                    ins.then_inc(getsem(o.eng), 1)
        for ch, v in chcount.items():
            nc.sync.wait_ge(getsem(ch), v)
        return nwait, len(ops), len(sems)


def _const_tables():
    ident = np.eye(128, dtype=np.float32)
    k = np.arange(128)[:, None]
    q = np.arange(128)[None, :]
    mcur = np.where(k <= q, 0.0, NEG).astype(np.float32)
    mprev = np.where(k > q, 0.0, NEG).astype(np.float32)
    masks = np.concatenate([np.tile(mprev, (1, 4)), np.tile(mcur, (1, 4))], axis=1)
    mats = []
    tp = np.arange(128)[:, None]
    t = np.arange(128)[None, :]
    for w in POOL_W:
        cur = np.where((tp <= t) & (tp > t - w), 1.0 / w, 0.0) - (tp == t)
        prev = np.where(tp - 128 > t - w, 1.0 / w, 0.0)
        cnt = np.minimum(t + 1, w).astype(np.float64)
        first = np.where((tp <= t) & (tp > t - w), 1.0 / cnt, 0.0) - (tp == t)
        mats += [cur, prev, first]
    mats = np.concatenate([m.astype(np.float32) for m in mats], axis=1)
    cbf = np.concatenate([ident, masks, mats], axis=1).astype(np.float32)
    inv_freq = (500000.0 ** (-(np.arange(0, 16, 2, dtype=np.float32)) / np.float32(16))).astype(np.float32)
    ang = (np.arange(SEQ, dtype=np.float32)[:, None] * inv_freq[None, :]).astype(np.float32)
    c = np.cos(ang).astype(np.float32)
    s = np.sin(ang).astype(np.float32)
    rope = np.concatenate([c, c, -s, s], axis=1).astype(np.float32)
    return ident, cbf, rope


def build(L, NT):
    nc = bass.Bass("TRN2", target_bir_lowering=False)
    S = Sched(nc)
    ntok = NT * TT

    def din(name, shape):
        return nc.dram_tensor(name, list(shape), F32, kind="ExternalInput").ap()

    x_d = din("x", [ntok, D])
    w_in_d = din("w_in", [L, D, D_IN])
    w_attn_d = din("w_attn_br", [L, D, D])
    w_rnn_d = din("w_rnn_br", [L, D, D])
    w_pool_d = din("w_pool_br", [L, D, D])
    w_out_d = din("w_out", [L, D, D])
    w_up_d = din("w_mlp_up", [L, D, D_FF])
    w_down_d = din("w_mlp_down", [L, D_FF, D])
    w_rga_d = din("w_rg_a", [L, 16, 64, 64])
    w_rgi_d = din("w_rg_i", [L, 16, 64, 64])
    w_pg_d = din("w_pool_groups", [L, 4, 256, 256])
    vecs_d = din("vecs", [L, 128, NV * 8])
    sink_d = din("sinks", [128, L * 8])
    identf_d = din("identf", [128, 128])
    cbf_d = din("cbf", [128, 2688])
    rope_d = din("rope", [ntok, 32])
    out_d = nc.dram_tensor("out", [ntok, D], F32, kind="ExternalOutput").ap()

    def dscr(name, shape):
        return nc.dram_tensor(name, list(shape), BF16, kind="Internal").ap()

    w_in_b = dscr("w_in_b", [L, D, D_IN])
    w_attn_b = dscr("w_attn_b", [L, D, D])
    w_rnn_b = dscr("w_rnn_b", [L, D, D])
    w_pool_b = dscr("w_pool_b", [L, D, D])
    w_out_b = dscr("w_out_b", [L, D, D])
    w_up_b = dscr("w_up_b", [L, D, D_FF])
    w_down_b = dscr("w_down_b", [L, D_FF, D])
    w_rga_b = dscr("w_rga_b", [L, 16, 64, 64])
    w_rgi_b = dscr("w_rgi_b", [L, 16, 64, 64])
    w_pg_b = dscr("w_pg_b", [L, 4, 256, 256])
    wg_b = dscr("wg_b", [L, 8, 128, 3072])
    wbr_b = dscr("wbr_b", [L, 8, 128, 3072])

    es = ExitStack()

    def sb(name, shape, dt):
        return es.enter_context(nc.sbuf_tensor("sb_" + name, list(shape), dt))

    with es:
        wslot = [sb("wslot%d" % i, [128, WSLOT_ELEMS], BF16) for i in range(NWSLOT)]
        R_wslot = [Res("wslot%d" % i) for i in range(NWSLOT)]
        xst = sb("xst", [128, NB, D], F32)
        R_xs = [Res("xs%d" % i) for i in range(8)]
        mixsb = xst[:].rearrange("p b (c t) -> p (b c) t", t=512)
        h = sb("h", [128, 8, TT], F32)
        R_h = [Res("h%d" % c) for c in range(8)]
        uT = sb("uT", [128, 8, TT], BF16)
        R_uT = [Res("uT%d" % c) for c in range(8)]
        qtm = sb("qtm", [128, NB, D], BF16)
        R_qtm = Res("qtm")
        pooledT = qtm[:].rearrange("p b (c t) -> p (b c) t", t=512)
        merged = pooledT
        ktm = sb("ktm", [128, NB, 256], BF16)
        R_ktm = [Res("ktm%d" % b) for b in range(NB)]
        vtm = sb("vtm", [128, NB + 1, 256], BF16)
        R_vtm = [Res("vtm%d" % b) for b in range(NB + 1)]
        ptm = sb("ptm", [128, NB + 1, D], BF16)
        R_ptm = [Res("ptm%d" % b) for b in range(NB + 1)]
        qT = sb("qT", [128, 8, TT], BF16)
        R_qT = [[Res("qT%d_%d" % (g, b)) for b in range(NB)] for g in range(4)]
        klo = sb("klo", [128, 4, 128 * (NB + 1)], BF16)
        khi = sb("khi", [128, 4, 128 * (NB + 1)], BF16)
        R_kTd = [Res("kTd%d" % b) for b in range(NB + 1)]
        pT = sb("pT", [128, 2, 2, 512], BF16)
        R_pT = [[Res("pT%d_%d" % (i, j)) for j in range(2)] for i in range(2)]
        rnnT = sb("rnnT", [128, 8, TT], BF16)
        R_rnnT = Res("rnnT")
        mixedT = sb("mixedT", [128, 8, TT], BF16)
        R_mixedT = Res("mixedT")
        th = sb("th", [128, 3, TT], F32)
        R_th = [Res("th%d" % i) for i in range(3)]
        rstd = sb("rstd", [128, TT], F32)
        R_rstd = Res("rstd")
        sq = sb("sq", [128, 2, TT], BF16)
        R_sq = [Res("sq0"), Res("sq1")]
        xrb = sb("xrb", [128, 2, TT + 3], F32)
        R_xrb = [Res("xrb0"), Res("xrb1")]
        yb = sb("yb", [128, 2, TT], F32)
        R_yb = [Res("yb0"), Res("yb1")]
        xc = sb("xc", [128, TT], F32); R_xc = Res("xc")
        xcb = sb("xcb", [128, TT], BF16); R_xcb = Res("xcb")
        tha = sb("tha", [128, TT], F32); R_tha = Res("tha")
        thi = sb("thi", [128, TT], F32); R_thi = Res("thi")
        aa = sb("aa", [128, TT], F32); R_aa = Res("aa")
        ss_ = sb("ss_", [128, TT], F32); R_ss = Res("ss_")
        x1 = sb("x1", [128, TT], F32); R_x1 = Res("x1")
        rl = sb("rl", [128, 2, TT], F32); R_rl = [Res("rl0"), Res("rl1")]
        ropet = sb("ropet", [128, NB, 32], F32); R_ropet = Res("ropet")
        rt1 = sb("rt1", [128, 2, 8, 16], F32); R_rt1 = [Res("rt1_0"), Res("rt1_1")]
        rt2 = sb("rt2", [128, 2, 8, 16], F32); R_rt2 = [Res("rt2_0"), Res("rt2_1")]
        att = sb("att", [128, 2, 256], F32); R_att = [Res("att0"), Res("att1")]
        identf = sb("identf", [128, 128], F32); R_const = Res("const")
        cbf = sb("cbf", [128, 2688], BF16)
        ones = sb("ones", [128, 128], BF16)
        cpow = sb("cpow", [128, 5], F32)
        vec = sb("vec", [128, L, NV, 8], F32)
        der = sb("der", [128, L, 4, 8], F32)
        esk = sb("esk", [128, L, 8], F32)
        wsm = sb("wsm", [128, 2, 4096], BF16)
        R_wsm = [Res("wsm0"), Res("wsm1")]
        kst = sb("kst", [128, L, 4, 128], BF16); R_kst = [Res("kst%d" % l) for l in range(L)]
        vst = sb("vst", [128, L, 256], BF16); R_vst = [Res("vst%d" % l) for l in range(L)]
        pst = sb("pst", [128, L, D], BF16); R_pst = [Res("pst%d" % l) for l in range(L)]
        ctail = sb("ctail", [128, L, 8, 3], F32); R_ctail = [Res("ctail%d" % l) for l in range(L)]
        hst = sb("hst", [128, L, 8], F32); R_hst = [Res("hst%d" % l) for l in range(L)]

        ident_bf = cbf[:, 0:128]
        mask_prev = cbf[:, 128:640]
        mask_cur = cbf[:, 640:1152]

        def poolmat(wi, kind):
            o = 1152 + (wi * 3 + kind) * 128
            return cbf[:, o:o + 128]

        banks = [es.enter_context(nc.psum_tensor("bank%d" % i, [128, 512], F32)) for i in range(8)]
        R_bank = [Res("bank%d" % i) for i in range(8)]
        bstate = {"i": 0}

        def nbank():
            i = bstate["i"]
            bstate["i"] = (i + 1) % 7
            return i
        SSB = 7

        R_scr = [Res("scr%d" % l) for l in range(L)]
        S.op("sp", lambda: [nc.sync.dma_start(out=identf[:], in_=identf_d),
                            nc.sync.dma_start(out=vec[:].rearrange("p l n c -> p l (n c)"), in_=vecs_d.rearrange("l p n -> p l n")),
                            nc.sync.dma_start(out=esk[:].rearrange("p l c -> p (l c)"), in_=sink_d)],
             writes=[R_const], dma=True, chan="c0")
        S.op("pool", lambda: [nc.gpsimd.dma_start(out=cbf[:], in_=cbf_d)], writes=[R_const], dma=True, chan="c1")
        for l in range(L):
            def castfn(l=l):
                ins = []
                for (dst, src, rows) in [(w_in_b, w_in_d, D), (w_out_b, w_out_d, D), (w_up_b, w_up_d, D),
                                         (w_down_b, w_down_d, D_FF)]:
                    for r0 in range(0, rows, 256):
                        ins.append(nc.gpsimd.dma_start(out=dst[l, r0:r0 + 256, :], in_=src[l, r0:r0 + 256, :]))
                for j in range(8):
                    for b_ in range(3):
                        dstg = wg_b[l, j].rearrange("p (k b c) -> p k b c", k=8, b=3)[:, :, b_, :]
                        srcg = w_in_d[l][:, C_GT + b_ * 1024 + j * 128:C_GT + b_ * 1024 + (j + 1) * 128].rearrange("(k p) c -> p k c", p=128)
                        ins.append(nc.gpsimd.dma_start(out=dstg, in_=srcg))
                        dstb = wbr_b[l, j].rearrange("p (k b c) -> p k b c", k=8, b=3)[:, :, b_, :]
                        srcb = (w_attn_d, w_rnn_d, w_pool_d)[b_][l][:, j * 128:(j + 1) * 128].rearrange("(k p) c -> p k c", p=128)
                        ins.append(nc.gpsimd.dma_start(out=dstb, in_=srcb))
                ins.append(nc.gpsimd.dma_start(out=w_rga_b[l].rearrange("a b c -> (a b) c"), in_=w_rga_d[l].rearrange("a b c -> (a b) c")))
                ins.append(nc.gpsimd.dma_start(out=w_rgi_b[l].rearrange("a b c -> (a b) c"), in_=w_rgi_d[l].rearrange("a b c -> (a b) c")))
                ins.append(nc.gpsimd.dma_start(out=w_pg_b[l].rearrange("a b c -> (a b) c"), in_=w_pg_d[l].rearrange("a b c -> (a b) c")))
                return ins
            S.op("pool", castfn, writes=[R_scr[l]], dma=True, chan="cast%d" % l)

        S.op("dve", nc.vector.memset, ones[:], 1.0, writes=[R_const])
        S.op("dve", nc.vector.memset, cpow[:, 0:1], -0.5, writes=[R_const])
        S.op("dve", nc.vector.memset, cpow[:, 1:2], 0.5, writes=[R_const])
        S.op("dve", nc.vector.memset, cpow[:, 2:3], EPS, writes=[R_const])
        S.op("dve", nc.vector.memset, cpow[:, 3:4], 0.7978845608, writes=[R_const])
        S.op("dve", nc.vector.memset, cpow[:, 4:5], 1.0, writes=[R_const])
        S.op("dve", nc.vector.memset, wsm[:], 0.0, writes=R_wsm)
        S.op("dve", nc.vector.memset, ctail[:], 0.0, writes=R_ctail)
        S.op("dve", nc.vector.memset, hst[:], 0.0, writes=R_hst)
        S.op("dve", nc.vector.memset, kst[:], 0.0, writes=R_kst)
        S.op("dve", nc.vector.memset, klo[:], 0.0, writes=R_kTd)
        S.op("dve", nc.vector.memset, khi[:], 0.0, writes=R_kTd)
        S.op("dve", nc.vector.memset, vst[:], 0.0, writes=R_vst)
        S.op("dve", nc.vector.memset, pst[:], 0.0, writes=R_pst)
        for l in range(L):
            S.op("dve", nc.vector.tensor_scalar, out=der[:, l, 0, :], in0=vec[:, l, 9, :], scalar1=0.5, scalar2=None, op0=ALU.mult, reads=[R_const], writes=[R_const])
            S.op("dve", nc.vector.tensor_scalar, out=der[:, l, 1, :], in0=vec[:, l, 10, :], scalar1=0.5, scalar2=None, op0=ALU.mult, reads=[R_const], writes=[R_const])
            S.op("act", nc.scalar.activation, out=der[:, l, 2, :], in_=vec[:, l, 11, :], func=AF.Exp, scale=-1.0, reads=[R_const], writes=[R_const])
            S.op("act", nc.scalar.activation, out=der[:, l, 2, :], in_=der[:, l, 2, :], func=AF.Ln, bias=1.0, scale=1.0, reads=[R_const], writes=[R_const])
            S.op("dve", nc.vector.tensor_scalar, out=der[:, l, 3, :], in0=der[:, l, 2, :], scalar1=-4.0, scalar2=None, op0=ALU.mult, reads=[R_const], writes=[R_const])
            S.op("dve", nc.vector.tensor_scalar, out=der[:, l, 2, :], in0=der[:, l, 2, :], scalar1=-8.0, scalar2=None, op0=ALU.mult, reads=[R_const], writes=[R_const])
        S.op("act", nc.scalar.activation, out=esk[:], in_=esk[:], func=AF.Exp, reads=[R_const], writes=[R_const])

        wstate = {"n": 0}

        def load_unit(pieces, l):
            n = wstate["n"]
            wstate["n"] = n + 1
            si = n % NWSLOT
            views = []
            off = 0
            dmas = []
            for src in pieces:
                rows, cols = src.shape
                kc = rows // 128
                dst = wslot[si][:, off:off + kc * cols].rearrange("p (k c) -> p k c", c=cols)
                views.append(dst)
                dmas.append((dst, src.rearrange("(k p) c -> p k c", p=128)))
                off += kc * cols
            assert off <= WSLOT_ELEMS
            S.op("sp", lambda: [nc.sync.dma_start(out=d_, in_=s_) for d_, s_ in dmas],
                 reads=[R_scr[l]], writes=[R_wslot[si]], dma=True, chan="w%d" % si)
            return views, R_wslot[si]

        def load_unit_raw(src, l):
            n = wstate["n"]
            wstate["n"] = n + 1
            si = n % NWSLOT
            ncol = src.shape[1]
            assert ncol <= WSLOT_ELEMS
            dst = wslot[si][:, 0:ncol]
            S.op("sp", lambda: [nc.sync.dma_start(out=dst, in_=src)], reads=[R_scr[l]], writes=[R_wslot[si]], dma=True, chan="w%d" % si)
            return dst, R_wslot[si]

        def prenorm(l, vi):
            for c in range(8):
                if c % 2 == 0:
                    S.op("act", nc.scalar.activation, out=sq[:, c % 2, :], in_=h[:, c, :], func=AF.Square,
                         reads=[R_h[c]], writes=[R_sq[c % 2]])
                else:
                    S.op("dve", nc.vector.tensor_tensor, out=sq[:, c % 2, :], in0=h[:, c, :], in1=h[:, c, :], op=ALU.mult,
                         reads=[R_h[c]], writes=[R_sq[c % 2]])
                S.op("pe", nc.tensor.matmul, banks[SSB][:], ones[:], sq[:, c % 2, :], start=(c == 0), stop=(c == 7),
                     reads=[R_sq[c % 2], R_const], writes=[R_bank[SSB]])
            S.op("act", nc.scalar.activation, out=rstd[:], in_=banks[SSB][:], func=AF.Ln, scale=1.0 / D, bias=cpow[:, 2:3],
                 reads=[R_bank[SSB], R_const], writes=[R_rstd])
            S.op("act", nc.scalar.activation, out=rstd[:], in_=rstd[:], func=AF.Exp, scale=-0.5,
                 reads=[R_rstd], writes=[R_rstd])
            for c in range(8):
                S.op("dve", nc.vector.scalar_tensor_tensor, out=uT[:, c, :], in0=h[:, c, :], scalar=vec[:, l, vi, c:c + 1], in1=rstd[:], op0=ALU.mult, op1=ALU.mult,
                     reads=[R_h[c], R_rstd, R_const], writes=[R_uT[c]])

        def postnorm(l, vi, srcfn, scale):
            for j in range(8):
                bi = srcfn(j)
                S.op("act", nc.scalar.activation, out=sq[:, j % 2, :], in_=banks[bi][:], func=AF.Square, scale=scale,
                     reads=[R_bank[bi]], writes=[R_sq[j % 2]])
                S.op("act", nc.scalar.activation, out=mixsb[:, j, :], in_=banks[bi][:], func=AF.Copy, scale=scale,
                     reads=[R_bank[bi]], writes=[R_xs[j]])
                S.op("pe", nc.tensor.matmul, banks[SSB][:], ones[:], sq[:, j % 2, :], start=(j == 0), stop=(j == 7),
                     reads=[R_sq[j % 2], R_const], writes=[R_bank[SSB]])
            S.op("act", nc.scalar.activation, out=rstd[:], in_=banks[SSB][:], func=AF.Ln, scale=1.0 / D, bias=cpow[:, 2:3],
                 reads=[R_bank[SSB], R_const], writes=[R_rstd])
            S.op("act", nc.scalar.activation, out=rstd[:], in_=rstd[:], func=AF.Exp, scale=-0.5,
                 reads=[R_rstd], writes=[R_rstd])
            for j in range(8):
                if j % 2 == 0:
                    S.op("pool", nc.gpsimd.tensor_tensor, out=mixsb[:, j, :], in0=mixsb[:, j, :], in1=rstd[:], op=ALU.mult,
                         reads=[R_xs[j], R_rstd], writes=[R_xs[j]])
                else:
                    S.op("dve", nc.vector.tensor_tensor, out=mixsb[:, j, :], in0=mixsb[:, j, :], in1=rstd[:], op=ALU.mult,
                         reads=[R_xs[j], R_rstd], writes=[R_xs[j]])
                S.op("dve", nc.vector.scalar_tensor_tensor, out=h[:, j, :], in0=mixsb[:, j, :], scalar=vec[:, l, vi, j:j + 1], in1=h[:, j, :], op0=ALU.mult, op1=ALU.add,
                     reads=[R_xs[j], R_h[j], R_const], writes=[R_h[j]])

        def fm_group(wview, Rw, col0, rhs_t, R_rhs, nk=8, k0=0, bank=None, start=True, stop=True):
            bi = nbank() if bank is None else bank
            for k in range(nk):
                S.op("pe", nc.tensor.matmul, banks[bi][:], wview[:, k, col0:col0 + 128], rhs_t[:, k0 + k, :],
                                                        start=(start and k == 0), stop=(stop and k == nk - 1),
                     reads=[Rw] + (R_rhs if isinstance(R_rhs, list) else [R_rhs]), writes=[R_bank[bi]])
            return bi

        def layer(t, l):
            gblock0 = (t == 0)
            win = w_in_b[l]
            ws = (t * L + l) % 2
            wv = wsm[:, ws, :]
            wrg = wv[:, 0:2048].rearrange("p (g c q) -> p g c q", g=2, c=8)
            wpg = wv[:, 2048:4096].rearrange("p (g k j) -> p g k j", g=4, k=2)

            def smallfn():
                ins = []
                for gi, src in enumerate((w_rga_b, w_rgi_b)):
                    s4 = src[l].rearrange("(c two) i j -> two i c j", two=2)
                    ins.append(nc.sync.dma_start(out=wrg[0:64, gi, :, 0:64], in_=s4[0]))
                    ins.append(nc.sync.dma_start(out=wrg[64:128, gi, :, 64:128], in_=s4[1]))
                ins.append(nc.sync.dma_start(out=wpg, in_=w_pg_b[l].rearrange("g (k p) j -> p g k j", p=128)))
                return ins
            S.op("sp", smallfn, reads=[R_scr[l]], writes=[R_wsm[ws]], dma=True, chan="wsm%d" % ws)


            S.lab = 'prenorm'
            prenorm(l, 0)

            S.lab = 'qkv'
            def tm_group(col0, evac):
                (wvw,), Rw = load_unit([win[:, col0:col0 + 512]], l)
                for b in range(NB):
                    bi = nbank()
                    for k in range(8):
                        S.op("pe", nc.tensor.matmul, banks[bi][:], uT[:, k, b * 128:(b + 1) * 128], wvw[:, k, :], start=(k == 0), stop=(k == 7),
                             reads=[Rw, R_uT[k]], writes=[R_bank[bi]])
                    evac(b, bi)

            def rope_evac(bank3, nh, dst3, b, Rdst, par):
                cc = ropet[:, b, 0:16].unsqueeze(1).to_broadcast([128, nh, 16])
                sneg = ropet[:, b, 16:24].unsqueeze(1).to_broadcast([128, nh, 8])
                spos = ropet[:, b, 24:32].unsqueeze(1).to_broadcast([128, nh, 8])
                t1 = rt1[:, par, 0:nh, :]
                t2 = rt2[:, par, 0:nh, :]
                bi_res = bank3[1]
                bk = bank3[0]
                S.op("dve", nc.vector.tensor_tensor, out=t1, in0=bk[:, :, 0:16], in1=cc, op=ALU.mult, reads=[bi_res, R_ropet], writes=[R_rt1[par]])
                S.op("dve", nc.vector.tensor_tensor, out=t2[:, :, 0:8], in0=bk[:, :, 8:16], in1=sneg, op=ALU.mult, reads=[bi_res, R_ropet], writes=[R_rt2[par]])
                S.op("dve", nc.vector.tensor_tensor, out=t2[:, :, 8:16], in0=bk[:, :, 0:8], in1=spos, op=ALU.mult, reads=[bi_res, R_ropet], writes=[R_rt2[par]])
                S.op("pool", nc.gpsimd.tensor_tensor, out=dst3[:, :, 0:16], in0=t1, in1=t2, op=ALU.add, reads=[R_rt1[par], R_rt2[par]], writes=[Rdst])
                S.op("act", nc.scalar.copy, out=dst3[:, :, 16:64], in_=bk[:, :, 16:64], reads=[bi_res], writes=[Rdst])

            def evac_q(half):
                def f(b, bi):
                    bk = banks[bi][:].rearrange("p (h d) -> p h d", d=64)
                    dst = qtm[:, b, half * 512:(half + 1) * 512].rearrange("p (h d) -> p h d", d=64)
                    rope_evac((bk, R_bank[bi]), 8, dst, b, R_qtm, half)
                return f

            def evac_kv(b, bi):
                bk = banks[bi][:, 0:256].rearrange("p (h d) -> p h d", d=64)
                dst = ktm[:, b, :].rearrange("p (h d) -> p h d", d=64)
                rope_evac((bk, R_bank[bi]), 4, dst, b, R_ktm[b], 0)
                S.op("act", nc.scalar.copy, out=vtm[:, b + 1, :], in_=banks[bi][:, 256:512], reads=[R_bank[bi]], writes=[R_vtm[b + 1]])

            tm_group(C_Q, evac_q(0))
            tm_group(C_Q + 512, evac_q(1))
            tm_group(C_K, evac_kv)

            S.lab = 'qkT'
            for b in range(NB):
                bi = nbank()
                bkb = banks[bi][:].bitcast(BF16)
                for c in range(8):
                    S.op("pe", nc.tensor.transpose, bkb[:, c * 128:(c + 1) * 128], qtm[:, b, c * 128:(c + 1) * 128], ident_bf,
                         reads=[R_qtm, R_const], writes=[R_bank[bi]])
                S.op("act", nc.scalar.copy, out=qT[:, :, b * 128:(b + 1) * 128], in_=bkb.rearrange("p (c t) -> p c t", t=128),
                     reads=[R_bank[bi]], writes=[R_qT[g][b] for g in range(4)])
                bi2 = nbank()
                bk2 = banks[bi2][:].bitcast(BF16)
                for g in range(4):
                    S.op("pe", nc.tensor.transpose, bk2[0:64, g * 128:(g + 1) * 128], ktm[:, b, g * 64:(g + 1) * 64], ident_bf,
                         reads=[R_ktm[b], R_const], writes=[R_bank[bi2]])
                    S.op("pe", nc.tensor.transpose, bk2[64:128, g * 128:(g + 1) * 128], ktm[:, b, g * 64:(g + 1) * 64], ident_bf,
                         reads=[R_ktm[b], R_const], writes=[R_bank[bi2]])
                S.op("dve", nc.vector.tensor_copy, out=klo[0:64, :, (b + 1) * 128:(b + 2) * 128], in_=bk2[0:64, 0:512].rearrange("p (g t) -> p g t", t=128),
                     reads=[R_bank[bi2]], writes=[R_kTd[b + 1]])
                S.op("act", nc.scalar.copy, out=khi[64:128, :, (b + 1) * 128:(b + 2) * 128], in_=bk2[64:128, 0:512].rearrange("p (g t) -> p g t", t=128),
                     reads=[R_bank[bi2]], writes=[R_kTd[b + 1]])

            S.lab = 'lru'
            LT = [
                dict(xc=xc[:], xcb=xcb[:], tha=tha[:], thi=thi[:], aa=aa[:], ss=ss_[:], x1=x1[:],
                     Rxc=[R_xc], Rxcb=[R_xcb], Rtha=[R_tha], Rthi=[R_thi], Raa=[R_aa], Rss=[R_ss], Rx1=[R_x1]),
                dict(xc=th[:, 0, :], xcb=sq[:, 1, :], tha=th[:, 1, :], thi=th[:, 2, :], aa=rl[:, 0, :], ss=rl[:, 1, :], x1=att[:].rearrange("p a b -> p (a b)"),
                     Rxc=[R_th[0]], Rxcb=[R_sq[1]], Rtha=[R_th[1]], Rthi=[R_th[2]], Raa=[R_rl[0]], Rss=[R_rl[1]], Rx1=[R_att[0], R_att[1]]),
            ]

            def lru_front(c, par, bx):
                T_ = LT[par]
                S.op("pool", nc.gpsimd.tensor_copy, out=xrb[:, par, 0:3], in_=ctail[:, l, c, :], reads=[R_ctail[l]], writes=[R_xrb[par]])
                S.op("act", nc.scalar.copy, out=xrb[:, par, 3:TT + 3], in_=banks[bx][:], reads=[R_bank[bx]], writes=[R_xrb[par]])
                S.op("pool", nc.gpsimd.tensor_copy, out=ctail[:, l, c, :], in_=xrb[:, par, TT:TT + 3], reads=[R_xrb[par]], writes=[R_ctail[l]])
                S.op("act", nc.scalar.activation, out=T_["xc"], in_=xrb[:, par, 0:TT], func=AF.Identity, scale=vec[:, l, 4, c:c + 1], bias=vec[:, l, 8, c:c + 1],
                     reads=[R_xrb[par], R_const], writes=T_["Rxc"])
                for tap in range(1, 4):
                    S.op("dve", nc.vector.scalar_tensor_tensor, out=T_["xc"], in0=xrb[:, par, tap:tap + TT], scalar=vec[:, l, 4 + tap, c:c + 1], in1=T_["xc"], op0=ALU.mult, op1=ALU.add,
                         reads=[R_xrb[par], R_const] + T_["Rxc"], writes=T_["Rxc"])
                S.op("pool", nc.gpsimd.tensor_copy, out=T_["xcb"], in_=T_["xc"], reads=T_["Rxc"], writes=T_["Rxcb"])

            def lru_y(c, par, by):
                S.op("dve", nc.vector.tensor_copy, out=yb[:, par, :], in_=banks[by][:], reads=[R_bank[by]], writes=[R_yb[par]])

            def lru_back(c, par):
                T_ = LT[par]
                ba = nbank()
                S.op("pe", nc.tensor.matmul, banks[ba][:], wrg[:, 0, c, :], T_["xcb"], start=True, stop=True, reads=[R_wsm[ws]] + T_["Rxcb"], writes=[R_bank[ba]])
                bi_ = nbank()
                S.op("pe", nc.tensor.matmul, banks[bi_][:], wrg[:, 1, c, :], T_["xcb"], start=True, stop=True, reads=[R_wsm[ws]] + T_["Rxcb"], writes=[R_bank[bi_]])
                S.op("act", nc.scalar.activation, out=T_["tha"], in_=banks[ba][:], func=AF.Tanh, bias=der[:, l, 0, c:c + 1], scale=0.5, reads=[R_bank[ba], R_const], writes=T_["Rtha"])
                S.op("act", nc.scalar.activation, out=T_["thi"], in_=banks[bi_][:], func=AF.Tanh, bias=der[:, l, 1, c:c + 1], scale=0.5, reads=[R_bank[bi_], R_const], writes=T_["Rthi"])
                S.op("act", nc.scalar.activation, out=T_["x1"], in_=yb[:, par, :], func=AF.Square, reads=[R_yb[par]], writes=T_["Rx1"])
                S.op("act", nc.scalar.activation, out=T_["x1"], in_=T_["x1"], func=AF.Identity, scale=0.0356774081, bias=cpow[:, 3:4], reads=T_["Rx1"] + [R_const], writes=T_["Rx1"])
                S.op("pool", nc.gpsimd.tensor_tensor, out=T_["x1"], in0=T_["x1"], in1=yb[:, par, :], op=ALU.mult, reads=T_["Rx1"] + [R_yb[par]], writes=T_["Rx1"])
                S.op("act", nc.scalar.activation, out=T_["aa"], in_=T_["tha"], func=AF.Exp, bias=der[:, l, 3, c:c + 1], scale=der[:, l, 3, c:c + 1], reads=T_["Rtha"] + [R_const], writes=T_["Raa"])
                S.op("act", nc.scalar.activation, out=T_["ss"], in_=T_["tha"], func=AF.Exp, bias=der[:, l, 2, c:c + 1], scale=der[:, l, 2, c:c + 1], reads=T_["Rtha"] + [R_const], writes=T_["Rss"])
                S.op("act", nc.scalar.activation, out=T_["x1"], in_=T_["x1"], func=AF.Tanh, reads=T_["Rx1"], writes=T_["Rx1"])
                S.op("act", nc.scalar.activation, out=T_["ss"], in_=T_["ss"], func=AF.Sqrt, scale=-1.0, bias=cpow[:, 4:5], reads=T_["Rss"] + [R_const], writes=T_["Rss"])
                S.op("dve", nc.vector.scalar_tensor_tensor, out=T_["thi"], in0=T_["thi"], scalar=1.0, in1=T_["xc"], op0=ALU.add, op1=ALU.mult, reads=T_["Rthi"] + T_["Rxc"], writes=T_["Rthi"])
                S.op("dve", nc.vector.tensor_tensor, out=T_["ss"], in0=T_["ss"], in1=T_["thi"], op=ALU.mult, reads=T_["Rss"] + T_["Rthi"], writes=T_["Rss"])
                S.op("dve", nc.vector.tensor_tensor_scan, out=T_["tha"], data0=T_["aa"], data1=T_["ss"], initial=hst[:, l, c:c + 1], op0=ALU.mult, op1=ALU.add,
                     reads=T_["Raa"] + T_["Rss"] + [R_hst[l]] + T_["Rtha"], writes=T_["Rtha"])
                S.op("pool", nc.gpsimd.tensor_copy, out=hst[:, l, c:c + 1], in_=T_["tha"][:, TT - 1:TT], reads=T_["Rtha"], writes=[R_hst[l]])
                S.op("dve", nc.vector.scalar_tensor_tensor, out=T_["x1"], in0=T_["x1"], scalar=1.0, in1=yb[:, par, :], op0=ALU.add, op1=ALU.mult, reads=T_["Rx1"] + [R_yb[par]], writes=T_["Rx1"])
                S.op("dve", nc.vector.scalar_tensor_tensor, out=rnnT[:, c, :], in0=T_["tha"], scalar=0.25, in1=T_["x1"], op0=ALU.mult, op1=ALU.mult, reads=T_["Rtha"] + T_["Rx1"], writes=[R_rnnT])

            prev_c = None
            for i in range(4):
                views, Rw = load_unit([win[:, C_XR + 256 * i:C_XR + 256 * (i + 1)], win[:, C_YR + 256 * i:C_YR + 256 * (i + 1)]], l)
                for cc_ in range(2):
                    c = 2 * i + cc_
                    par = c % 2
                    bx = fm_group(views[0], Rw, cc_ * 128, uT, R_uT)
                    lru_front(c, par, bx)
                    by = fm_group(views[1], Rw, cc_ * 128, uT, R_uT)
                    lru_y(c, par, by)
                    if prev_c is not None:
                        lru_back(prev_c, prev_c % 2)
                    prev_c = c
            lru_back(prev_c, prev_c % 2)

            S.op("pool", nc.gpsimd.tensor_copy, out=ptm[:, 0, :], in_=pst[:, l, :], reads=[R_pst[l]], writes=[R_ptm[0]])
            S.lab = 'pp'
            def evac_p(half):
                def f(b, bi):
                    eng = "dve" if (b % 2 == 0) else "act"
                    if eng == "dve":
                        S.op("dve", nc.vector.tensor_copy, out=ptm[:, b + 1, half * 512:(half + 1) * 512], in_=banks[bi][:], reads=[R_bank[bi]], writes=[R_ptm[b + 1]])
                    else:
                        S.op("act", nc.scalar.copy, out=ptm[:, b + 1, half * 512:(half + 1) * 512], in_=banks[bi][:], reads=[R_bank[bi]], writes=[R_ptm[b + 1]])
                return f
            tm_group(C_PP, evac_p(0))
            tm_group(C_PP + 512, evac_p(1))

            S.op("pool", nc.gpsimd.tensor_copy, out=klo[0:64, :, 0:128], in_=kst[0:64, l, :, :], reads=[R_kst[l]], writes=[R_kTd[0]])
            S.op("pool", nc.gpsimd.tensor_copy, out=khi[64:128, :, 0:128], in_=kst[64:128, l, :, :], reads=[R_kst[l]], writes=[R_kTd[0]])
            S.op("pool", nc.gpsimd.tensor_copy, out=vtm[:, 0, :], in_=vst[:, l, :], reads=[R_vst[l]], writes=[R_vtm[0]])
            S.lab = 'attn'
            def att_A(g, b):
                first = gblock0 and b == 0
                pb = (g * NB + b) % 2
                kinds = []
                for kind in ((1,) if first else (0, 1)):
                    bi = nbank()
                    kinds.append(kind)
                    kb = b + kind
                    msk = mask_cur if kind == 1 else mask_prev
                    S.lab = 'mask'
                    S.op("pe", nc.tensor.matmul, banks[bi][:], ident_bf, msk, start=True, stop=False, reads=[R_const], writes=[R_bank[bi]])
                    S.lab = 'attn'
                    S.op("pe", nc.tensor.matmul, banks[bi][:, 0:256], klo[:, g, kb * 128:(kb + 1) * 128], qT[:, 2 * g:2 * g + 2, b * 128:(b + 1) * 128], start=False, stop=False,
                         reads=[R_kTd[kb], R_qT[g][b]], writes=[R_bank[bi]])
                    S.op("pe", nc.tensor.matmul, banks[bi][:, 256:512], khi[:, g, kb * 128:(kb + 1) * 128], qT[:, 2 * g:2 * g + 2, b * 128:(b + 1) * 128], start=False, stop=True,
                         reads=[R_kTd[kb], R_qT[g][b]], writes=[R_bank[bi]])
                    S.op("act", nc.scalar.activation, out=pT[:, pb, kind, :], in_=banks[bi][:], func=AF.Exp, scale=0.125, reads=[R_bank[bi]], writes=[R_pT[pb][kind]])
                return kinds

            def att_B(g, b, kinds):
                pb = (g * NB + b) % 2
                nd = nbank()
                for half in range(2):
                    ps = slice(0, 64) if half == 0 else slice(64, 128)
                    cs = slice(0, 256) if half == 0 else slice(256, 512)
                    for which in range(2):
                        ocs = slice(0, 256) if which == 0 else slice(256, 512)
                        for ii, kind in enumerate(kinds):
                            kb = b + kind
                            if which == 0:
                                S.op("pe", nc.tensor.matmul, banks[nd][ps, ocs], vtm[:, kb, g * 64:(g + 1) * 64], pT[:, pb, kind, cs], start=(ii == 0), stop=(ii == len(kinds) - 1),
                                     reads=[R_vtm[kb], R_pT[pb][kind]], writes=[R_bank[nd]])
                            else:
                                S.op("pe", nc.tensor.matmul, banks[nd][ps, ocs], ones[:, 0:64], pT[:, pb, kind, cs], start=(ii == 0), stop=(ii == len(kinds) - 1),
                                     reads=[R_const, R_pT[pb][kind]], writes=[R_bank[nd]])
                ek = esk[:, l, 2 * g:2 * g + 2].unsqueeze(2).to_broadcast([128, 2, 128])
                S.op("dve", nc.vector.tensor_tensor, out=att[:, pb, :].rearrange("p (c t) -> p c t", t=128), in0=banks[nd][:, 256:512].rearrange("p (c t) -> p c t", t=128), in1=ek, op=ALU.add,
                     reads=[R_bank[nd], R_const], writes=[R_att[pb]])
                S.op("dve", nc.vector.reciprocal, out=att[:, pb, :], in_=att[:, pb, :], reads=[R_att[pb]], writes=[R_att[pb]])
                S.op("dve", nc.vector.tensor_tensor, out=qT[:, 2 * g:2 * g + 2, b * 128:(b + 1) * 128], in0=banks[nd][:, 0:256].rearrange("p (c t) -> p c t", t=128), in1=att[:, pb, :].rearrange("p (c t) -> p c t", t=128), op=ALU.mult,
                     reads=[R_bank[nd], R_att[pb]], writes=[R_qT[g][b]])

            units = [(g, b) for g in range(4) for b in range(NB)]
            pend = None
            for (g, b) in units:
                kinds = att_A(g, b)
                if pend is not None:
                    att_B(*pend)
                pend = (g, b, kinds)
            att_B(*pend)
            S.op("pool", nc.gpsimd.tensor_copy, out=kst[0:64, l, :, :], in_=klo[0:64, :, NB * 128:(NB + 1) * 128], reads=[R_kTd[NB]], writes=[R_kst[l]])
            S.op("pool", nc.gpsimd.tensor_copy, out=kst[64:128, l, :, :], in_=khi[64:128, :, NB * 128:(NB + 1) * 128], reads=[R_kTd[NB]], writes=[R_kst[l]])
            S.op("pool", nc.gpsimd.tensor_copy, out=vst[:, l, :], in_=vtm[:, NB, :], reads=[R_vtm[NB]], writes=[R_vst[l]])

            S.lab = 'pool'
            for c in range(8):
                wi = c // 2
                bi = nbank()
                for b in range(NB):
                    first = gblock0 and b == 0
                    S.op("pe", nc.tensor.matmul, banks[bi][:, b * 128:(b + 1) * 128], ptm[:, b + 1, c * 128:(c + 1) * 128], poolmat(wi, 2 if first else 0), start=True, stop=first,
                         reads=[R_ptm[b + 1], R_const], writes=[R_bank[bi]])
                    if not first:
                        S.op("pe", nc.tensor.matmul, banks[bi][:, b * 128:(b + 1) * 128], ptm[:, b, c * 128:(c + 1) * 128], poolmat(wi, 1), start=False, stop=True,
                             reads=[R_ptm[b], R_const], writes=[R_bank[bi]])
                S.op("act", nc.scalar.copy, out=pooledT[:, c, :], in_=banks[bi][:], reads=[R_bank[bi]], writes=[R_qtm])
            S.op("pool", nc.gpsimd.tensor_copy, out=pst[:, l, :], in_=ptm[:, NB, :], reads=[R_ptm[NB]], writes=[R_pst[l]])
            for j in range(8):
                gi, jj = j // 2, j % 2
                bi = nbank()
                for k in range(2):
                    S.op("pe", nc.tensor.matmul, banks[bi][:], wpg[:, gi, k, jj * 128:(jj + 1) * 128], pooledT[:, 2 * gi + k, :], start=(k == 0), stop=(k == 1),
                         reads=[R_wsm[ws], R_qtm], writes=[R_bank[bi]])
                S.op("dve", nc.vector.tensor_scalar, out=mixedT[:, j, :], in0=banks[bi][:], scalar1=vec[:, l, 12, j:j + 1], scalar2=None, op0=ALU.mult,
                     reads=[R_bank[bi], R_const], writes=[R_mixedT])

            S.lab = 'merge'
            R_qT_all = [R_qT[g][b] for g in range(4) for b in range(NB)]
            for j in range(8):
                gsl, Rg = load_unit_raw(wg_b[l, j], l)
                gv = [gsl.rearrange("p (k b c) -> p k b c", k=8, b=3)[:, :, bi_, :] for bi_ in range(3)]
                gb = []
                for bi_ in range(3):
                    bk = fm_group(gv[bi_], Rg, 0, uT, R_uT)
                    gb.append(bk)
                    S.op("act", nc.scalar.activation, out=th[:, bi_, :], in_=banks[bk][:], func=AF.Tanh, scale=0.5, reads=[R_bank[bk]], writes=[R_th[bi_]])
                bsl, Rb = load_unit_raw(wbr_b[l, j], l)
                bv = [bsl.rearrange("p (k b c) -> p k b c", k=8, b=3)[:, :, bi_, :] for bi_ in range(3)]
                for bi_, (src, Rsrc) in enumerate(((qT, R_qT_all), (rnnT, R_rnnT), (mixedT, R_mixedT))):
                    bk = fm_group(bv[bi_], Rb, 0, src, Rsrc)
                    S.op("dve", nc.vector.scalar_tensor_tensor, out=th[:, bi_, :], in0=th[:, bi_, :], scalar=1.0, in1=banks[bk][:], op0=ALU.add, op1=ALU.mult,
                         reads=[R_th[bi_], R_bank[bk]], writes=[R_th[bi_]])
                S.op("pool", nc.gpsimd.tensor_tensor, out=th[:, 0, :], in0=th[:, 0, :], in1=th[:, 1, :], op=ALU.add, reads=[R_th[0], R_th[1]], writes=[R_th[0]])
                S.op("pool", nc.gpsimd.tensor_tensor, out=merged[:, j, :], in0=th[:, 0, :], in1=th[:, 2, :], op=ALU.add, reads=[R_th[0], R_th[2]], writes=[R_qtm])

            S.lab = 'wout'
            wo = {}

            def wout_src(j):
                if j % 4 == 0:
                    wo["v"], wo["R"] = load_unit([w_out_b[l][:, (j // 4) * 512:(j // 4 + 1) * 512]], l)
                return fm_group(wo["v"][0], wo["R"], (j % 4) * 128, merged, R_qtm)
            postnorm(l, 1, wout_src, 0.5)

            S.lab = 'mlp_pre'
            prenorm(l, 2)
            actT = [qT[:, i, :] for i in range(8)] + [rnnT[:, i, :] for i in range(8)] + [mixedT[:, i, :] for i in range(8)] + [pooledT[:, i, :] for i in range(8)]
            R_act = [R_qT_all] * 8 + [[R_rnnT]] * 8 + [[R_mixedT]] * 8 + [[R_qtm]] * 8
            S.lab = 'up'
            for f in range(32):
                if f % 4 == 0:
                    (uv,), Ru = load_unit([w_up_b[l][:, (f // 4) * 512:(f // 4 + 1) * 512]], l)
                bk = fm_group(uv, Ru, (f % 4) * 128, uT, R_uT)
                rp = f % 2
                S.op("act", nc.scalar.activation, out=rl[:, rp, :], in_=banks[bk][:], func=AF.Relu, reads=[R_bank[bk]], writes=[R_rl[rp]])
                if f % 2 == 0:
                    S.op("dve", nc.vector.tensor_tensor, out=actT[f], in0=rl[:, rp, :], in1=rl[:, rp, :], op=ALU.mult, reads=[R_rl[rp]], writes=R_act[f])
                else:
                    S.op("pool", nc.gpsimd.tensor_tensor, out=actT[f], in0=rl[:, rp, :], in1=rl[:, rp, :], op=ALU.mult, reads=[R_rl[rp]], writes=R_act[f])
            S.lab = 'down'
            dn = {}

            def down_src(j):
                jp, jj = j // 2, j % 2
                if jj == 0:
                    dn["b"] = [nbank(), nbank()]
                    for half in range(2):
                        (dv,), Rd = load_unit([w_down_b[l][half * 2048:(half + 1) * 2048, jp * 256:(jp + 1) * 256]], l)
                        for k in range(16):
                            for j2 in range(2):
                                kk = half * 16 + k
                                S.op("pe", nc.tensor.matmul, banks[dn["b"][j2]][:], dv[:, k, j2 * 128:(j2 + 1) * 128], actT[kk], start=(kk == 0), stop=(kk == 31),
                                     reads=[Rd] + R_act[kk], writes=[R_bank[dn["b"][j2]]])
                return dn["b"][jj]
            postnorm_down(l, down_src)

        def postnorm_down(l, down_src):
            postnorm(l, 3, down_src, 1.0)

        for t in range(NT):
            S.lab = 'xin'
            S.op("sp", lambda t=t: [nc.sync.dma_start(out=xst[:], in_=x_d[t * TT:(t + 1) * TT, :].rearrange("(b p) d -> p b d", p=128)),
                                    nc.sync.dma_start(out=ropet[:], in_=rope_d[t * TT:(t + 1) * TT, :].rearrange("(b p) n -> p b n", p=128))],
                 writes=R_xs + [R_ropet], dma=True, chan="xin")
            for c in range(8):
                bi = nbank()
                for b in range(NB):
                    S.op("pe", nc.tensor.transpose, banks[bi][:, b * 128:(b + 1) * 128], xst[:, b, c * 128:(c + 1) * 128], identf[:],
                         reads=[R_xs[b * 2 + c // 4], R_const], writes=[R_bank[bi]])
                if c % 2 == 0:
                    S.op("dve", nc.vector.tensor_copy, out=h[:, c, :], in_=banks[bi][:], reads=[R_bank[bi]], writes=[R_h[c]])
                else:
                    S.op("act", nc.scalar.copy, out=h[:, c, :], in_=banks[bi][:], reads=[R_bank[bi]], writes=[R_h[c]])
            for l in range(L):
                layer(t, l)
            S.lab = 'xout'
            for b in range(NB):
                for half in range(2):
                    bi = nbank()
                    for cc_ in range(4):
                        c = half * 4 + cc_
                        S.op("pe", nc.tensor.transpose, banks[bi][:, cc_ * 128:(cc_ + 1) * 128], h[:, c, b * 128:(b + 1) * 128], identf[:],
                             reads=[R_h[c], R_const], writes=[R_bank[bi]])
                    if half == 0:
                        S.op("dve", nc.vector.tensor_copy, out=xst[:, b, half * 512:(half + 1) * 512], in_=banks[bi][:], reads=[R_bank[bi]], writes=[R_xs[b * 2 + half]])
                    else:
                        S.op("act", nc.scalar.copy, out=xst[:, b, half * 512:(half + 1) * 512], in_=banks[bi][:], reads=[R_bank[bi]], writes=[R_xs[b * 2 + half]])
            S.op("sp", lambda t=t: [nc.sync.dma_start(out=out_d[t * TT:(t + 1) * TT, :].rearrange("(b p) d -> p b d", p=128), in_=xst[:])],
                 reads=R_xs, dma=True, chan="xout")
        fin = Res("fin")
        S.op("sp", nc.sync.nop, reads=R_xs, writes=[fin] + R_xs)
        import os
        if os.environ.get('KLABELS'):
            open(os.environ['KLABELS'], 'w').write('\n'.join(o.lab for o in S.ops if o.eng == 'pe' and not o.isdma and o.lab != 'mask'))
        stats = S.emit(es)
    return nc, stats


_VEC_ORDER = ["norm_mix_pre", "norm_mix_post", "norm_mlp_pre", "norm_mlp_post", None, None, None, None,
              "conv_b", "b_rg_a", "b_rg_i", "lru_lambda", "pool_scale"]


def _pack_layer_inputs(inp, layers):
    L = len(layers)
    vecs = np.zeros((L, 128, NV, 8), np.float32)
    for li, l in enumerate(layers):
        for vi, name in enumerate(_VEC_ORDER):
            if name is None:
                v = np.asarray(inp["conv_w"])[l, vi - 4]
            else:
                v = np.asarray(inp[name])[l]
            vecs[li, :, vi, :] = v.reshape(8, 128).T
    sinks = np.zeros((128, L, 8), np.float32)
    for li, l in enumerate(layers):
        s = np.asarray(inp["attn_sinks"])[l]
        sinks[0:64, li, :] = s[0::2][None, :]
        sinks[64:128, li, :] = s[1::2][None, :]
    m = {"vecs": vecs.reshape(L, 128, NV * 8), "sinks": sinks.reshape(128, L * 8)}
    for name in ["w_in", "w_attn_br", "w_rnn_br", "w_pool_br", "w_out", "w_mlp_up", "w_mlp_down", "w_rg_a", "w_rg_i", "w_pool_groups"]:
        m[name] = np.ascontiguousarray(np.asarray(inp[name])[layers], dtype=np.float32)
    return m


_CACHE = {}


def _get_prog(L, NT):
    key = (L, NT)
    if key not in _CACHE:
        _CACHE[key] = build(L, NT)[0]
    return _CACHE[key]


def run_layers(x_all, inp, layers, ncores=NCORE, NT=SEQ // TT):
    identf, cbf, rope = _const_tables()
    nc = _get_prog(len(layers), NT)
    base = _pack_layer_inputs(inp, layers)
    base["identf"] = identf
    base["cbf"] = cbf
    base["rope"] = np.ascontiguousarray(rope[:NT * TT])
    in_maps = []
    for c in range(ncores):
        m = dict(base)
        m["x"] = np.ascontiguousarray(x_all[c], dtype=np.float32)
        in_maps.append(m)
    res = run_bass_kernel_spmd(nc, in_maps, core_ids=list(range(ncores)))
    return np.stack([np.asarray(r["out"]) for r in res.results], axis=0)


def kernel(**inputs):
    x = np.asarray(inputs["x"], dtype=np.float32)
    if FUSED:
        out = run_layers(x, inputs, list(range(DEPTH)))
    else:
        out = x
        for l in range(DEPTH):
            out = run_layers(out, inputs, [l])
    return out.astype(np.float32)
```

```python
import numpy as np
from contextlib import ExitStack
import concourse.bass as bass
import concourse.mybir as mybir
from concourse.bass_utils import run_bass_kernel_spmd

F32 = mybir.dt.float32
BF16 = mybir.dt.bfloat16
AF = mybir.ActivationFunctionType
ALU = mybir.AluOpType

D = 1024
SEQ = 8192
NCORE = 8
DEPTH = 4
TT = 512
NB = TT // 128
D_IN = 7680
D_FF = 4096
EPS = 1e-6
NV = 13
NEG = -30000.0
WSLOT_ELEMS = 4096
NWSLOT = 3
import os as _os
SAME_ENGINE_SYNC = _os.environ.get('KSES', '1') == '1'
FUSED = True

C_Q, C_K, C_V, C_XR, C_YR, C_PP, C_GT = 0, 1024, 1280, 1536, 2560, 3584, 4608
POOL_W = (2, 4, 8, 16)


class Res:
    __slots__ = ("name", "lw", "rd")

    def __init__(self, name):
        self.name = name
        self.lw = None
        self.rd = {}


class Op:
    __slots__ = ("eng", "fn", "deps", "sig", "tok", "chan", "isdma", "lab")

    def __init__(self, eng, fn, deps, chan, isdma):
        self.eng = eng
        self.fn = fn
        self.deps = deps
        self.sig = False
        self.tok = None
        self.chan = chan
        self.isdma = isdma


class Sched:
    def __init__(self, nc):
        self.nc = nc
        self.ops = []
        self.eng = {"pe": nc.tensor, "act": nc.scalar, "dve": nc.vector, "pool": nc.gpsimd, "sp": nc.sync}

    def op(self, eng, meth, *args, reads=(), writes=(), dma=False, chan=None, **kw):
        idx = len(self.ops)
        deps = set()
        for r in reads:
            if r.lw is not None:
                deps.add(r.lw)
        for w in writes:
            if w.lw is not None:
                deps.add(w.lw)
            deps.update(w.rd.values())
        deps.discard(idx)
        if dma:
            fn = meth
        else:
            fn = (meth, args, kw)
        self.ops.append(Op(eng, fn, deps, chan, dma))
        self.ops[-1].lab = getattr(self, 'lab', '')
        key = ("dma", idx) if dma else eng
        for r in reads:
            r.rd[key] = idx
        for w in writes:
            w.lw = idx
            w.rd = {}
        return idx

    def emit(self, es):
        nc = self.nc
        ops = self.ops
        for i, o in enumerate(ops):
            keep = set()
            for d in o.deps:
                od = ops[d]
                if not od.isdma and not o.isdma and od.eng == o.eng:
                    if o.eng == "pe" or not SAME_ENGINE_SYNC:
                        continue
                keep.add(d)
            o.deps = keep
            for d in keep:
                ops[d].sig = True
        sems = {}

        def getsem(key):
            if key not in sems:
                sems[key] = es.enter_context(nc.semaphore("s_" + str(key)))
            return sems[key]

        count = {}
        for o in ops:
            if o.isdma:
                continue
            if o.sig:
                count[o.eng] = count.get(o.eng, 0) + 1
                o.tok = (o.eng, count[o.eng])
        waited = {e: {} for e in self.eng}
        chcount = {}
        nwait = 0
        import os
        lim = int(os.environ.get('KLIMIT', '0')) or len(ops)
        for oi, o in enumerate(ops):
            if oi >= lim:
                break
            e = self.eng[o.eng]
            need = {}
            for d in o.deps:
                k, v = ops[d].tok
                if need.get(k, 0) < v:
                    need[k] = v
            for k, v in need.items():
                if waited[o.eng].get(k, 0) < v:
                    e.wait_ge(getsem(k), v)
                    waited[o.eng][k] = v
                    nwait += 1
            if o.isdma:
                ins = o.fn()
                if not isinstance(ins, (list, tuple)):
                    ins = [ins]
                s = getsem(o.chan)
                for i_ in ins:
                    i_.then_inc(s, 16)
                chcount[o.chan] = chcount.get(o.chan, 0) + 16 * len(ins)
                o.tok = (o.chan, chcount[o.chan])
            else:
                meth, args, kw = o.fn
                ins = meth(*args, **kw)
                if o.sig:
                    ins.then_inc(getsem(o.eng), 1)
        for ch, v in chcount.items():
            nc.sync.wait_ge(getsem(ch), v)
        return nwait, len(ops), len(sems)


def _const_tables():
    ident = np.eye(128, dtype=np.float32)
    k = np.arange(128)[:, None]
    q = np.arange(128)[None, :]
    mcur = np.where(k <= q, 0.0, NEG).astype(np.float32)
    mprev = np.where(k > q, 0.0, NEG).astype(np.float32)
    masks = np.concatenate([np.tile(mprev, (1, 4)), np.tile(mcur, (1, 4))], axis=1)
    mats = []
    tp = np.arange(128)[:, None]
    t = np.arange(128)[None, :]
    for w in POOL_W:
        cur = np.where((tp <= t) & (tp > t - w), 1.0 / w, 0.0) - (tp == t)
        prev = np.where(tp - 128 > t - w, 1.0 / w, 0.0)
        cnt = np.minimum(t + 1, w).astype(np.float64)
        first = np.where((tp <= t) & (tp > t - w), 1.0 / cnt, 0.0) - (tp == t)
        mats += [cur, prev, first]
    mats = np.concatenate([m.astype(np.float32) for m in mats], axis=1)
    cbf = np.concatenate([ident, masks, mats], axis=1).astype(np.float32)
    inv_freq = (500000.0 ** (-(np.arange(0, 16, 2, dtype=np.float32)) / np.float32(16))).astype(np.float32)
    ang = (np.arange(SEQ, dtype=np.float32)[:, None] * inv_freq[None, :]).astype(np.float32)
    c = np.cos(ang).astype(np.float32)
    s = np.sin(ang).astype(np.float32)
    rope = np.concatenate([c, c, -s, s], axis=1).astype(np.float32)
    return ident, cbf, rope


def build(L, NT):
    nc = bass.Bass("TRN2", target_bir_lowering=False)
    S = Sched(nc)
    ntok = NT * TT

    def din(name, shape):
        return nc.dram_tensor(name, list(shape), F32, kind="ExternalInput").ap()

    x_d = din("x", [ntok, D])
    w_in_d = din("w_in", [L, D, D_IN])
    w_attn_d = din("w_attn_br", [L, D, D])
    w_rnn_d = din("w_rnn_br", [L, D, D])
    w_pool_d = din("w_pool_br", [L, D, D])
    w_out_d = din("w_out", [L, D, D])
    w_up_d = din("w_mlp_up", [L, D, D_FF])
    w_down_d = din("w_mlp_down", [L, D_FF, D])
    w_rga_d = din("w_rg_a", [L, 16, 64, 64])
    w_rgi_d = din("w_rg_i", [L, 16, 64, 64])
    w_pg_d = din("w_pool_groups", [L, 4, 256, 256])
    vecs_d = din("vecs", [L, 128, NV * 8])
    sink_d = din("sinks", [128, L * 8])
    identf_d = din("identf", [128, 128])
    cbf_d = din("cbf", [128, 2688])
    rope_d = din("rope", [ntok, 32])
    out_d = nc.dram_tensor("out", [ntok, D], F32, kind="ExternalOutput").ap()

    def dscr(name, shape):
        return nc.dram_tensor(name, list(shape), BF16, kind="Internal").ap()

    w_in_b = dscr("w_in_b", [L, D, D_IN])
    w_attn_b = dscr("w_attn_b", [L, D, D])
    w_rnn_b = dscr("w_rnn_b", [L, D, D])
    w_pool_b = dscr("w_pool_b", [L, D, D])
    w_out_b = dscr("w_out_b", [L, D, D])
    w_up_b = dscr("w_up_b", [L, D, D_FF])
    w_down_b = dscr("w_down_b", [L, D_FF, D])
    w_rga_b = dscr("w_rga_b", [L, 16, 64, 64])
    w_rgi_b = dscr("w_rgi_b", [L, 16, 64, 64])
    w_pg_b = dscr("w_pg_b", [L, 4, 256, 256])
    wg_b = dscr("wg_b", [L, 8, 128, 3072])
    wbr_b = dscr("wbr_b", [L, 8, 128, 3072])

    es = ExitStack()

    def sb(name, shape, dt):
        return es.enter_context(nc.sbuf_tensor("sb_" + name, list(shape), dt))

    with es:
        wslot = [sb("wslot%d" % i, [128, WSLOT_ELEMS], BF16) for i in range(NWSLOT)]
        R_wslot = [Res("wslot%d" % i) for i in range(NWSLOT)]
        xst = sb("xst", [128, NB, D], F32)
        R_xs = [Res("xs%d" % i) for i in range(8)]
        mixsb = xst[:].rearrange("p b (c t) -> p (b c) t", t=512)
        h = sb("h", [128, 8, TT], F32)
        R_h = [Res("h%d" % c) for c in range(8)]
        uT = sb("uT", [128, 8, TT], BF16)
        R_uT = [Res("uT%d" % c) for c in range(8)]
        qtm = sb("qtm", [128, NB, D], BF16)
        R_qtm = Res("qtm")
        pooledT = qtm[:].rearrange("p b (c t) -> p (b c) t", t=512)
        merged = pooledT
        ktm = sb("ktm", [128, NB, 256], BF16)
        R_ktm = [Res("ktm%d" % b) for b in range(NB)]
        vtm = sb("vtm", [128, NB + 1, 256], BF16)
        R_vtm = [Res("vtm%d" % b) for b in range(NB + 1)]
        ptm = sb("ptm", [128, NB + 1, D], BF16)
        R_ptm = [Res("ptm%d" % b) for b in range(NB + 1)]
        qT = sb("qT", [128, 8, TT], BF16)
        R_qT = [[Res("qT%d_%d" % (g, b)) for b in range(NB)] for g in range(4)]
        klo = sb("klo", [128, 4, 128 * (NB + 1)], BF16)
        khi = sb("khi", [128, 4, 128 * (NB + 1)], BF16)
        R_kTd = [Res("kTd%d" % b) for b in range(NB + 1)]
        pT = sb("pT", [128, 2, 2, 512], BF16)
        R_pT = [[Res("pT%d_%d" % (i, j)) for j in range(2)] for i in range(2)]
        rnnT = sb("rnnT", [128, 8, TT], BF16)
        R_rnnT = Res("rnnT")
        mixedT = sb("mixedT", [128, 8, TT], BF16)
        R_mixedT = Res("mixedT")
        th = sb("th", [128, 3, TT], F32)
        R_th = [Res("th%d" % i) for i in range(3)]
        rstd = sb("rstd", [128, TT], F32)
        R_rstd = Res("rstd")
        sq = sb("sq", [128, 2, TT], BF16)
        R_sq = [Res("sq0"), Res("sq1")]
        xrb = sb("xrb", [128, 2, TT + 3], F32)
        R_xrb = [Res("xrb0"), Res("xrb1")]
        yb = sb("yb", [128, 2, TT], F32)
        R_yb = [Res("yb0"), Res("yb1")]
        xc = sb("xc", [128, TT], F32); R_xc = Res("xc")
        xcb = sb("xcb", [128, TT], BF16); R_xcb = Res("xcb")
        tha = sb("tha", [128, TT], F32); R_tha = Res("tha")
        thi = sb("thi", [128, TT], F32); R_thi = Res("thi")
        aa = sb("aa", [128, TT], F32); R_aa = Res("aa")
        ss_ = sb("ss_", [128, TT], F32); R_ss = Res("ss_")
        x1 = sb("x1", [128, TT], F32); R_x1 = Res("x1")
        rl = sb("rl", [128, 2, TT], F32); R_rl = [Res("rl0"), Res("rl1")]
        ropet = sb("ropet", [128, NB, 32], F32); R_ropet = Res("ropet")
        rt1 = sb("rt1", [128, 2, 8, 16], F32); R_rt1 = [Res("rt1_0"), Res("rt1_1")]
        rt2 = sb("rt2", [128, 2, 8, 16], F32); R_rt2 = [Res("rt2_0"), Res("rt2_1")]
        att = sb("att", [128, 2, 256], F32); R_att = [Res("att0"), Res("att1")]
        identf = sb("identf", [128, 128], F32); R_const = Res("const")
        cbf = sb("cbf", [128, 2688], BF16)
        ones = sb("ones", [128, 128], BF16)
        cpow = sb("cpow", [128, 5], F32)
        vec = sb("vec", [128, L, NV, 8], F32)
        der = sb("der", [128, L, 4, 8], F32)
        esk = sb("esk", [128, L, 8], F32)
        wsm = sb("wsm", [128, 2, 4096], BF16)
        R_wsm = [Res("wsm0"), Res("wsm1")]
        kst = sb("kst", [128, L, 4, 128], BF16); R_kst = [Res("kst%d" % l) for l in range(L)]
        vst = sb("vst", [128, L, 256], BF16); R_vst = [Res("vst%d" % l) for l in range(L)]
        pst = sb("pst", [128, L, D], BF16); R_pst = [Res("pst%d" % l) for l in range(L)]
        ctail = sb("ctail", [128, L, 8, 3], F32); R_ctail = [Res("ctail%d" % l) for l in range(L)]
        hst = sb("hst", [128, L, 8], F32); R_hst = [Res("hst%d" % l) for l in range(L)]

        ident_bf = cbf[:, 0:128]
        mask_prev = cbf[:, 128:640]
        mask_cur = cbf[:, 640:1152]

        def poolmat(wi, kind):
            o = 1152 + (wi * 3 + kind) * 128
            return cbf[:, o:o + 128]

        banks = [es.enter_context(nc.psum_tensor("bank%d" % i, [128, 512], F32)) for i in range(8)]
        R_bank = [Res("bank%d" % i) for i in range(8)]
        bstate = {"i": 0}

        def nbank():
            i = bstate["i"]
            bstate["i"] = (i + 1) % 7
            return i
        SSB = 7

        R_scr = [Res("scr%d" % l) for l in range(L)]
        S.op("sp", lambda: [nc.sync.dma_start(out=identf[:], in_=identf_d),
                            nc.sync.dma_start(out=vec[:].rearrange("p l n c -> p l (n c)"), in_=vecs_d.rearrange("l p n -> p l n")),
                            nc.sync.dma_start(out=esk[:].rearrange("p l c -> p (l c)"), in_=sink_d)],
             writes=[R_const], dma=True, chan="c0")
        S.op("pool", lambda: [nc.gpsimd.dma_start(out=cbf[:], in_=cbf_d)], writes=[R_const], dma=True, chan="c1")
        for l in range(L):
            def castfn(l=l):
                ins = []
                for (dst, src, rows) in [(w_in_b, w_in_d, D), (w_out_b, w_out_d, D), (w_up_b, w_up_d, D),
                                         (w_down_b, w_down_d, D_FF)]:
                    for r0 in range(0, rows, 256):
                        ins.append(nc.gpsimd.dma_start(out=dst[l, r0:r0 + 256, :], in_=src[l, r0:r0 + 256, :]))
                for j in range(8):
                    for b_ in range(3):
                        dstg = wg_b[l, j].rearrange("p (k b c) -> p k b c", k=8, b=3)[:, :, b_, :]
                        srcg = w_in_d[l][:, C_GT + b_ * 1024 + j * 128:C_GT + b_ * 1024 + (j + 1) * 128].rearrange("(k p) c -> p k c", p=128)
                        ins.append(nc.gpsimd.dma_start(out=dstg, in_=srcg))
                        dstb = wbr_b[l, j].rearrange("p (k b c) -> p k b c", k=8, b=3)[:, :, b_, :]
                        srcb = (w_attn_d, w_rnn_d, w_pool_d)[b_][l][:, j * 128:(j + 1) * 128].rearrange("(k p) c -> p k c", p=128)
                        ins.append(nc.gpsimd.dma_start(out=dstb, in_=srcb))
                ins.append(nc.gpsimd.dma_start(out=w_rga_b[l].rearrange("a b c -> (a b) c"), in_=w_rga_d[l].rearrange("a b c -> (a b) c")))
                ins.append(nc.gpsimd.dma_start(out=w_rgi_b[l].rearrange("a b c -> (a b) c"), in_=w_rgi_d[l].rearrange("a b c -> (a b) c")))
                ins.append(nc.gpsimd.dma_start(out=w_pg_b[l].rearrange("a b c -> (a b) c"), in_=w_pg_d[l].rearrange("a b c -> (a b) c")))
                return ins
            S.op("pool", castfn, writes=[R_scr[l]], dma=True, chan="cast%d" % l)

        S.op("dve", nc.vector.memset, ones[:], 1.0, writes=[R_const])
        S.op("dve", nc.vector.memset, cpow[:, 0:1], -0.5, writes=[R_const])
        S.op("dve", nc.vector.memset, cpow[:, 1:2], 0.5, writes=[R_const])
        S.op("dve", nc.vector.memset, cpow[:, 2:3], EPS, writes=[R_const])
        S.op("dve", nc.vector.memset, cpow[:, 3:4], 0.7978845608, writes=[R_const])
        S.op("dve", nc.vector.memset, cpow[:, 4:5], 1.0, writes=[R_const])
        S.op("dve", nc.vector.memset, wsm[:], 0.0, writes=R_wsm)
        S.op("dve", nc.vector.memset, ctail[:], 0.0, writes=R_ctail)
        S.op("dve", nc.vector.memset, hst[:], 0.0, writes=R_hst)
        S.op("dve", nc.vector.memset, kst[:], 0.0, writes=R_kst)
        S.op("dve", nc.vector.memset, klo[:], 0.0, writes=R_kTd)
        S.op("dve", nc.vector.memset, khi[:], 0.0, writes=R_kTd)
        S.op("dve", nc.vector.memset, vst[:], 0.0, writes=R_vst)
        S.op("dve", nc.vector.memset, pst[:], 0.0, writes=R_pst)
        for l in range(L):
            S.op("dve", nc.vector.tensor_scalar, out=der[:, l, 0, :], in0=vec[:, l, 9, :], scalar1=0.5, scalar2=None, op0=ALU.mult, reads=[R_const], writes=[R_const])
            S.op("dve", nc.vector.tensor_scalar, out=der[:, l, 1, :], in0=vec[:, l, 10, :], scalar1=0.5, scalar2=None, op0=ALU.mult, reads=[R_const], writes=[R_const])
            S.op("act", nc.scalar.activation, out=der[:, l, 2, :], in_=vec[:, l, 11, :], func=AF.Exp, scale=-1.0, reads=[R_const], writes=[R_const])
            S.op("act", nc.scalar.activation, out=der[:, l, 2, :], in_=der[:, l, 2, :], func=AF.Ln, bias=1.0, scale=1.0, reads=[R_const], writes=[R_const])
            S.op("dve", nc.vector.tensor_scalar, out=der[:, l, 3, :], in0=der[:, l, 2, :], scalar1=-4.0, scalar2=None, op0=ALU.mult, reads=[R_const], writes=[R_const])
            S.op("dve", nc.vector.tensor_scalar, out=der[:, l, 2, :], in0=der[:, l, 2, :], scalar1=-8.0, scalar2=None, op0=ALU.mult, reads=[R_const], writes=[R_const])
        S.op("act", nc.scalar.activation, out=esk[:], in_=esk[:], func=AF.Exp, reads=[R_const], writes=[R_const])

        wstate = {"n": 0}

        def load_unit(pieces, l):
            n = wstate["n"]
            wstate["n"] = n + 1
            si = n % NWSLOT
            views = []
            off = 0
            dmas = []
            for src in pieces:
                rows, cols = src.shape
                kc = rows // 128
                dst = wslot[si][:, off:off + kc * cols].rearrange("p (k c) -> p k c", c=cols)
                views.append(dst)
                dmas.append((dst, src.rearrange("(k p) c -> p k c", p=128)))
                off += kc * cols
            assert off <= WSLOT_ELEMS
            S.op("sp", lambda: [nc.sync.dma_start(out=d_, in_=s_) for d_, s_ in dmas],
                 reads=[R_scr[l]], writes=[R_wslot[si]], dma=True, chan="w%d" % si)
            return views, R_wslot[si]

        def load_unit_raw(src, l):
            n = wstate["n"]
            wstate["n"] = n + 1
            si = n % NWSLOT
            ncol = src.shape[1]
            assert ncol <= WSLOT_ELEMS
            dst = wslot[si][:, 0:ncol]
            S.op("sp", lambda: [nc.sync.dma_start(out=dst, in_=src)], reads=[R_scr[l]], writes=[R_wslot[si]], dma=True, chan="w%d" % si)
            return dst, R_wslot[si]

        def prenorm(l, vi):
            for c in range(8):
                if c % 2 == 0:
                    S.op("act", nc.scalar.activation, out=sq[:, c % 2, :], in_=h[:, c, :], func=AF.Square,
                         reads=[R_h[c]], writes=[R_sq[c % 2]])
                else:
                    S.op("dve", nc.vector.tensor_tensor, out=sq[:, c % 2, :], in0=h[:, c, :], in1=h[:, c, :], op=ALU.mult,
                         reads=[R_h[c]], writes=[R_sq[c % 2]])
                S.op("pe", nc.tensor.matmul, banks[SSB][:], ones[:], sq[:, c % 2, :], start=(c == 0), stop=(c == 7),
                     reads=[R_sq[c % 2], R_const], writes=[R_bank[SSB]])
            S.op("act", nc.scalar.activation, out=rstd[:], in_=banks[SSB][:], func=AF.Ln, scale=1.0 / D, bias=cpow[:, 2:3],
                 reads=[R_bank[SSB], R_const], writes=[R_rstd])
            S.op("act", nc.scalar.activation, out=rstd[:], in_=rstd[:], func=AF.Exp, scale=-0.5,
                 reads=[R_rstd], writes=[R_rstd])
            for c in range(8):
                S.op("dve", nc.vector.scalar_tensor_tensor, out=uT[:, c, :], in0=h[:, c, :], scalar=vec[:, l, vi, c:c + 1], in1=rstd[:], op0=ALU.mult, op1=ALU.mult,
                     reads=[R_h[c], R_rstd, R_const], writes=[R_uT[c]])

        def postnorm(l, vi, srcfn, scale):
            for j in range(8):
                bi = srcfn(j)
                S.op("act", nc.scalar.activation, out=sq[:, j % 2, :], in_=banks[bi][:], func=AF.Square, scale=scale,
                     reads=[R_bank[bi]], writes=[R_sq[j % 2]])
                S.op("act", nc.scalar.activation, out=mixsb[:, j, :], in_=banks[bi][:], func=AF.Copy, scale=scale,
                     reads=[R_bank[bi]], writes=[R_xs[j]])
                S.op("pe", nc.tensor.matmul, banks[SSB][:], ones[:], sq[:, j % 2, :], start=(j == 0), stop=(j == 7),
                     reads=[R_sq[j % 2], R_const], writes=[R_bank[SSB]])
            S.op("act", nc.scalar.activation, out=rstd[:], in_=banks[SSB][:], func=AF.Ln, scale=1.0 / D, bias=cpow[:, 2:3],
                 reads=[R_bank[SSB], R_const], writes=[R_rstd])
            S.op("act", nc.scalar.activation, out=rstd[:], in_=rstd[:], func=AF.Exp, scale=-0.5,
                 reads=[R_rstd], writes=[R_rstd])
            for j in range(8):
                if j % 2 == 0:
                    S.op("pool", nc.gpsimd.tensor_tensor, out=mixsb[:, j, :], in0=mixsb[:, j, :], in1=rstd[:], op=ALU.mult,
                         reads=[R_xs[j], R_rstd], writes=[R_xs[j]])
                else:
                    S.op("dve", nc.vector.tensor_tensor, out=mixsb[:, j, :], in0=mixsb[:, j, :], in1=rstd[:], op=ALU.mult,
                         reads=[R_xs[j], R_rstd], writes=[R_xs[j]])
                S.op("dve", nc.vector.scalar_tensor_tensor, out=h[:, j, :], in0=mixsb[:, j, :], scalar=vec[:, l, vi, j:j + 1], in1=h[:, j, :], op0=ALU.mult, op1=ALU.add,
                     reads=[R_xs[j], R_h[j], R_const], writes=[R_h[j]])

        def fm_group(wview, Rw, col0, rhs_t, R_rhs, nk=8, k0=0, bank=None, start=True, stop=True):
            bi = nbank() if bank is None else bank
            for k in range(nk):
                S.op("pe", nc.tensor.matmul, banks[bi][:], wview[:, k, col0:col0 + 128], rhs_t[:, k0 + k, :],
                                                        start=(start and k == 0), stop=(stop and k == nk - 1),
                     reads=[Rw] + (R_rhs if isinstance(R_rhs, list) else [R_rhs]), writes=[R_bank[bi]])
            return bi

        def layer(t, l):
            gblock0 = (t == 0)
            win = w_in_b[l]
            ws = (t * L + l) % 2
            wv = wsm[:, ws, :]
            wrg = wv[:, 0:2048].rearrange("p (g c q) -> p g c q", g=2, c=8)
            wpg = wv[:, 2048:4096].rearrange("p (g k j) -> p g k j", g=4, k=2)

            def smallfn():
                ins = []
                for gi, src in enumerate((w_rga_b, w_rgi_b)):
                    s4 = src[l].rearrange("(c two) i j -> two i c j", two=2)
                    ins.append(nc.sync.dma_start(out=wrg[0:64, gi, :, 0:64], in_=s4[0]))
                    ins.append(nc.sync.dma_start(out=wrg[64:128, gi, :, 64:128], in_=s4[1]))
                ins.append(nc.sync.dma_start(out=wpg, in_=w_pg_b[l].rearrange("g (k p) j -> p g k j", p=128)))
                return ins
            S.op("sp", smallfn, reads=[R_scr[l]], writes=[R_wsm[ws]], dma=True, chan="wsm%d" % ws)


            S.lab = 'prenorm'
            prenorm(l, 0)

            S.lab = 'qkv'
            def tm_group(col0, evac):
                (wvw,), Rw = load_unit([win[:, col0:col0 + 512]], l)
                for b in range(NB):
                    bi = nbank()
                    for k in range(8):
                        S.op("pe", nc.tensor.matmul, banks[bi][:], uT[:, k, b * 128:(b + 1) * 128], wvw[:, k, :], start=(k == 0), stop=(k == 7),
                             reads=[Rw, R_uT[k]], writes=[R_bank[bi]])
                    evac(b, bi)

            def rope_evac(bank3, nh, dst3, b, Rdst, par):
                cc = ropet[:, b, 0:16].unsqueeze(1).to_broadcast([128, nh, 16])
                sneg = ropet[:, b, 16:24].unsqueeze(1).to_broadcast([128, nh, 8])
                spos = ropet[:, b, 24:32].unsqueeze(1).to_broadcast([128, nh, 8])
                t1 = rt1[:, par, 0:nh, :]
                t2 = rt2[:, par, 0:nh, :]
                bi_res = bank3[1]
                bk = bank3[0]
                S.op("dve", nc.vector.tensor_tensor, out=t1, in0=bk[:, :, 0:16], in1=cc, op=ALU.mult, reads=[bi_res, R_ropet], writes=[R_rt1[par]])
                S.op("dve", nc.vector.tensor_tensor, out=t2[:, :, 0:8], in0=bk[:, :, 8:16], in1=sneg, op=ALU.mult, reads=[bi_res, R_ropet], writes=[R_rt2[par]])
                S.op("dve", nc.vector.tensor_tensor, out=t2[:, :, 8:16], in0=bk[:, :, 0:8], in1=spos, op=ALU.mult, reads=[bi_res, R_ropet], writes=[R_rt2[par]])
                S.op("pool", nc.gpsimd.tensor_tensor, out=dst3[:, :, 0:16], in0=t1, in1=t2, op=ALU.add, reads=[R_rt1[par], R_rt2[par]], writes=[Rdst])
                S.op("act", nc.scalar.copy, out=dst3[:, :, 16:64], in_=bk[:, :, 16:64], reads=[bi_res], writes=[Rdst])

            def evac_q(half):
                def f(b, bi):
                    bk = banks[bi][:].rearrange("p (h d) -> p h d", d=64)
                    dst = qtm[:, b, half * 512:(half + 1) * 512].rearrange("p (h d) -> p h d", d=64)
                    rope_evac((bk, R_bank[bi]), 8, dst, b, R_qtm, half)
                return f

            def evac_kv(b, bi):
                bk = banks[bi][:, 0:256].rearrange("p (h d) -> p h d", d=64)
                dst = ktm[:, b, :].rearrange("p (h d) -> p h d", d=64)
                rope_evac((bk, R_bank[bi]), 4, dst, b, R_ktm[b], 0)
                S.op("act", nc.scalar.copy, out=vtm[:, b + 1, :], in_=banks[bi][:, 256:512], reads=[R_bank[bi]], writes=[R_vtm[b + 1]])

            tm_group(C_Q, evac_q(0))
            tm_group(C_Q + 512, evac_q(1))
            tm_group(C_K, evac_kv)

            S.lab = 'qkT'
            for b in range(NB):
                bi = nbank()
                bkb = banks[bi][:].bitcast(BF16)
                for c in range(8):
                    S.op("pe", nc.tensor.transpose, bkb[:, c * 128:(c + 1) * 128], qtm[:, b, c * 128:(c + 1) * 128], ident_bf,
                         reads=[R_qtm, R_const], writes=[R_bank[bi]])
                S.op("act", nc.scalar.copy, out=qT[:, :, b * 128:(b + 1) * 128], in_=bkb.rearrange("p (c t) -> p c t", t=128),
                     reads=[R_bank[bi]], writes=[R_qT[g][b] for g in range(4)])
                bi2 = nbank()
                bk2 = banks[bi2][:].bitcast(BF16)
                for g in range(4):
                    S.op("pe", nc.tensor.transpose, bk2[0:64, g * 128:(g + 1) * 128], ktm[:, b, g * 64:(g + 1) * 64], ident_bf,
                         reads=[R_ktm[b], R_const], writes=[R_bank[bi2]])
                    S.op("pe", nc.tensor.transpose, bk2[64:128, g * 128:(g + 1) * 128], ktm[:, b, g * 64:(g + 1) * 64], ident_bf,
                         reads=[R_ktm[b], R_const], writes=[R_bank[bi2]])
                S.op("dve", nc.vector.tensor_copy, out=klo[0:64, :, (b + 1) * 128:(b + 2) * 128], in_=bk2[0:64, 0:512].rearrange("p (g t) -> p g t", t=128),
                     reads=[R_bank[bi2]], writes=[R_kTd[b + 1]])
                S.op("act", nc.scalar.copy, out=khi[64:128, :, (b + 1) * 128:(b + 2) * 128], in_=bk2[64:128, 0:512].rearrange("p (g t) -> p g t", t=128),
                     reads=[R_bank[bi2]], writes=[R_kTd[b + 1]])

            S.op("pool", nc.gpsimd.tensor_copy, out=ptm[:, 0, :], in_=pst[:, l, :], reads=[R_pst[l]], writes=[R_ptm[0]])
            S.lab = 'pp'
            def evac_p(half):
                def f(b, bi):
                    eng = "dve" if (b % 2 == 0) else "act"
                    if eng == "dve":
                        S.op("dve", nc.vector.tensor_copy, out=ptm[:, b + 1, half * 512:(half + 1) * 512], in_=banks[bi][:], reads=[R_bank[bi]], writes=[R_ptm[b + 1]])
                    else:
                        S.op("act", nc.scalar.copy, out=ptm[:, b + 1, half * 512:(half + 1) * 512], in_=banks[bi][:], reads=[R_bank[bi]], writes=[R_ptm[b + 1]])
                return f
            tm_group(C_PP, evac_p(0))
            tm_group(C_PP + 512, evac_p(1))

            S.op("pool", nc.gpsimd.tensor_copy, out=klo[0:64, :, 0:128], in_=kst[0:64, l, :, :], reads=[R_kst[l]], writes=[R_kTd[0]])
            S.op("pool", nc.gpsimd.tensor_copy, out=khi[64:128, :, 0:128], in_=kst[64:128, l, :, :], reads=[R_kst[l]], writes=[R_kTd[0]])
            S.op("pool", nc.gpsimd.tensor_copy, out=vtm[:, 0, :], in_=vst[:, l, :], reads=[R_vst[l]], writes=[R_vtm[0]])
            S.lab = 'attn'
            def att_A(g, b):
                first = gblock0 and b == 0
                pb = (g * NB + b) % 2
                kinds = []
                for kind in ((1,) if first else (0, 1)):
                    bi = nbank()
                    kinds.append(kind)
                    kb = b + kind
                    msk = mask_cur if kind == 1 else mask_prev
                    S.lab = 'mask'
                    S.op("pe", nc.tensor.matmul, banks[bi][:], ident_bf, msk, start=True, stop=False, reads=[R_const], writes=[R_bank[bi]])
                    S.lab = 'attn'
                    S.op("pe", nc.tensor.matmul, banks[bi][:, 0:256], klo[:, g, kb * 128:(kb + 1) * 128], qT[:, 2 * g:2 * g + 2, b * 128:(b + 1) * 128], start=False, stop=False,
                         reads=[R_kTd[kb], R_qT[g][b]], writes=[R_bank[bi]])
                    S.op("pe", nc.tensor.matmul, banks[bi][:, 256:512], khi[:, g, kb * 128:(kb + 1) * 128], qT[:, 2 * g:2 * g + 2, b * 128:(b + 1) * 128], start=False, stop=True,
                         reads=[R_kTd[kb], R_qT[g][b]], writes=[R_bank[bi]])
                    S.op("act", nc.scalar.activation, out=pT[:, pb, kind, :], in_=banks[bi][:], func=AF.Exp, scale=0.125, reads=[R_bank[bi]], writes=[R_pT[pb][kind]])
                return kinds

            def att_B(g, b, kinds):
                pb = (g * NB + b) % 2
                nd = nbank()
                for half in range(2):
                    ps = slice(0, 64) if half == 0 else slice(64, 128)
                    cs = slice(0, 256) if half == 0 else slice(256, 512)
                    for which in range(2):
                        ocs = slice(0, 256) if which == 0 else slice(256, 512)
                        for ii, kind in enumerate(kinds):
                            kb = b + kind
                            if which == 0:
                                S.op("pe", nc.tensor.matmul, banks[nd][ps, ocs], vtm[:, kb, g * 64:(g + 1) * 64], pT[:, pb, kind, cs], start=(ii == 0), stop=(ii == len(kinds) - 1),
                                     reads=[R_vtm[kb], R_pT[pb][kind]], writes=[R_bank[nd]])
                            else:
                                S.op("pe", nc.tensor.matmul, banks[nd][ps, ocs], ones[:, 0:64], pT[:, pb, kind, cs], start=(ii == 0), stop=(ii == len(kinds) - 1),
                                     reads=[R_const, R_pT[pb][kind]], writes=[R_bank[nd]])
                ek = esk[:, l, 2 * g:2 * g + 2].unsqueeze(2).to_broadcast([128, 2, 128])
                S.op("dve", nc.vector.tensor_tensor, out=att[:, pb, :].rearrange("p (c t) -> p c t", t=128), in0=banks[nd][:, 256:512].rearrange("p (c t) -> p c t", t=128), in1=ek, op=ALU.add,
                     reads=[R_bank[nd], R_const], writes=[R_att[pb]])
                S.op("dve", nc.vector.reciprocal, out=att[:, pb, :], in_=att[:, pb, :], reads=[R_att[pb]], writes=[R_att[pb]])
                S.op("dve", nc.vector.tensor_tensor, out=qT[:, 2 * g:2 * g + 2, b * 128:(b + 1) * 128], in0=banks[nd][:, 0:256].rearrange("p (c t) -> p c t", t=128), in1=att[:, pb, :].rearrange("p (c t) -> p c t", t=128), op=ALU.mult,
                     reads=[R_bank[nd], R_att[pb]], writes=[R_qT[g][b]])

            units = [(g, b) for g in range(4) for b in range(NB)]
            ast = {"i": 0, "pend": None}

            def att_step():
                if ast["i"] < len(units):
                    g, b = units[ast["i"]]
                    ast["i"] += 1
                    S.lab = 'attn'
                    kinds = att_A(g, b)
                    if ast["pend"] is not None:
                        att_B(*ast["pend"])
                    ast["pend"] = (g, b, kinds)
                    S.lab = 'lru'
            S.lab = 'lru'
            LT = [
                dict(xc=xc[:], xcb=xcb[:], tha=tha[:], thi=thi[:], aa=aa[:], ss=ss_[:], x1=x1[:],
                     Rxc=[R_xc], Rxcb=[R_xcb], Rtha=[R_tha], Rthi=[R_thi], Raa=[R_aa], Rss=[R_ss], Rx1=[R_x1]),
                dict(xc=th[:, 0, :], xcb=sq[:, 1, :], tha=th[:, 1, :], thi=th[:, 2, :], aa=rl[:, 0, :], ss=rl[:, 1, :], x1=mixedT[:, 0:2, :].rearrange("p a t -> p (a t)").bitcast(F32),
                     Rxc=[R_th[0]], Rxcb=[R_sq[1]], Rtha=[R_th[1]], Rthi=[R_th[2]], Raa=[R_rl[0]], Rss=[R_rl[1]], Rx1=[R_mixedT]),
            ]

            def lru_front(c, par, bx):
                T_ = LT[par]
                S.op("pool", nc.gpsimd.tensor_copy, out=xrb[:, par, 0:3], in_=ctail[:, l, c, :], reads=[R_ctail[l]], writes=[R_xrb[par]])
                S.op("act", nc.scalar.copy, out=xrb[:, par, 3:TT + 3], in_=banks[bx][:], reads=[R_bank[bx]], writes=[R_xrb[par]])
                S.op("pool", nc.gpsimd.tensor_copy, out=ctail[:, l, c, :], in_=xrb[:, par, TT:TT + 3], reads=[R_xrb[par]], writes=[R_ctail[l]])
                S.op("act", nc.scalar.activation, out=T_["xc"], in_=xrb[:, par, 0:TT], func=AF.Identity, scale=vec[:, l, 4, c:c + 1], bias=vec[:, l, 8, c:c + 1],
                     reads=[R_xrb[par], R_const], writes=T_["Rxc"])
                for tap in range(1, 4):
                    S.op("dve", nc.vector.scalar_tensor_tensor, out=T_["xc"], in0=xrb[:, par, tap:tap + TT], scalar=vec[:, l, 4 + tap, c:c + 1], in1=T_["xc"], op0=ALU.mult, op1=ALU.add,
                         reads=[R_xrb[par], R_const] + T_["Rxc"], writes=T_["Rxc"])
                S.op("pool", nc.gpsimd.tensor_copy, out=T_["xcb"], in_=T_["xc"], reads=T_["Rxc"], writes=T_["Rxcb"])

            def lru_y(c, par, by):
                S.op("dve", nc.vector.tensor_copy, out=yb[:, par, :], in_=banks[by][:], reads=[R_bank[by]], writes=[R_yb[par]])

            def lru_back(c, par):
                T_ = LT[par]
                ba = nbank()
                S.op("pe", nc.tensor.matmul, banks[ba][:], wrg[:, 0, c, :], T_["xcb"], start=True, stop=True, reads=[R_wsm[ws]] + T_["Rxcb"], writes=[R_bank[ba]])
                bi_ = nbank()
                S.op("pe", nc.tensor.matmul, banks[bi_][:], wrg[:, 1, c, :], T_["xcb"], start=True, stop=True, reads=[R_wsm[ws]] + T_["Rxcb"], writes=[R_bank[bi_]])
                S.op("act", nc.scalar.activation, out=T_["tha"], in_=banks[ba][:], func=AF.Tanh, bias=der[:, l, 0, c:c + 1], scale=0.5, reads=[R_bank[ba], R_const], writes=T_["Rtha"])
                S.op("act", nc.scalar.activation, out=T_["thi"], in_=banks[bi_][:], func=AF.Tanh, bias=der[:, l, 1, c:c + 1], scale=0.5, reads=[R_bank[bi_], R_const], writes=T_["Rthi"])
                S.op("act", nc.scalar.activation, out=T_["x1"], in_=yb[:, par, :], func=AF.Square, reads=[R_yb[par]], writes=T_["Rx1"])
                S.op("act", nc.scalar.activation, out=T_["x1"], in_=T_["x1"], func=AF.Identity, scale=0.0356774081, bias=cpow[:, 3:4], reads=T_["Rx1"] + [R_const], writes=T_["Rx1"])
                S.op("pool", nc.gpsimd.tensor_tensor, out=T_["x1"], in0=T_["x1"], in1=yb[:, par, :], op=ALU.mult, reads=T_["Rx1"] + [R_yb[par]], writes=T_["Rx1"])
                S.op("act", nc.scalar.activation, out=T_["aa"], in_=T_["tha"], func=AF.Exp, bias=der[:, l, 3, c:c + 1], scale=der[:, l, 3, c:c + 1], reads=T_["Rtha"] + [R_const], writes=T_["Raa"])
                S.op("act", nc.scalar.activation, out=T_["ss"], in_=T_["tha"], func=AF.Exp, bias=der[:, l, 2, c:c + 1], scale=der[:, l, 2, c:c + 1], reads=T_["Rtha"] + [R_const], writes=T_["Rss"])
                S.op("act", nc.scalar.activation, out=T_["x1"], in_=T_["x1"], func=AF.Tanh, reads=T_["Rx1"], writes=T_["Rx1"])
                S.op("act", nc.scalar.activation, out=T_["ss"], in_=T_["ss"], func=AF.Sqrt, scale=-1.0, bias=cpow[:, 4:5], reads=T_["Rss"] + [R_const], writes=T_["Rss"])
                S.op("dve", nc.vector.scalar_tensor_tensor, out=T_["thi"], in0=T_["thi"], scalar=1.0, in1=T_["xc"], op0=ALU.add, op1=ALU.mult, reads=T_["Rthi"] + T_["Rxc"], writes=T_["Rthi"])
                S.op("dve", nc.vector.tensor_tensor, out=T_["ss"], in0=T_["ss"], in1=T_["thi"], op=ALU.mult, reads=T_["Rss"] + T_["Rthi"], writes=T_["Rss"])
                S.op("dve", nc.vector.tensor_tensor_scan, out=T_["tha"], data0=T_["aa"], data1=T_["ss"], initial=hst[:, l, c:c + 1], op0=ALU.mult, op1=ALU.add,
                     reads=T_["Raa"] + T_["Rss"] + [R_hst[l]] + T_["Rtha"], writes=T_["Rtha"])
                S.op("pool", nc.gpsimd.tensor_copy, out=hst[:, l, c:c + 1], in_=T_["tha"][:, TT - 1:TT], reads=T_["Rtha"], writes=[R_hst[l]])
                S.op("dve", nc.vector.scalar_tensor_tensor, out=T_["x1"], in0=T_["x1"], scalar=1.0, in1=yb[:, par, :], op0=ALU.add, op1=ALU.mult, reads=T_["Rx1"] + [R_yb[par]], writes=T_["Rx1"])
                S.op("dve", nc.vector.scalar_tensor_tensor, out=rnnT[:, c, :], in0=T_["tha"], scalar=0.25, in1=T_["x1"], op0=ALU.mult, op1=ALU.mult, reads=T_["Rtha"] + T_["Rx1"], writes=[R_rnnT])

            prev_c = None
            for i in range(4):
                views, Rw = load_unit([win[:, C_XR + 256 * i:C_XR + 256 * (i + 1)], win[:, C_YR + 256 * i:C_YR + 256 * (i + 1)]], l)
                for cc_ in range(2):
                    c = 2 * i + cc_
                    par = c % 2
                    bx = fm_group(views[0], Rw, cc_ * 128, uT, R_uT)
                    lru_front(c, par, bx)
                    by = fm_group(views[1], Rw, cc_ * 128, uT, R_uT)
                    lru_y(c, par, by)
                    if prev_c is not None:
                        lru_back(prev_c, prev_c % 2)
                    prev_c = c
                    att_step()
                    att_step()
            lru_back(prev_c, prev_c % 2)
            while ast["i"] < len(units):
                att_step()
            S.lab = 'attn'
            att_B(*ast["pend"])

            S.op("pool", nc.gpsimd.tensor_copy, out=kst[0:64, l, :, :], in_=klo[0:64, :, NB * 128:(NB + 1) * 128], reads=[R_kTd[NB]], writes=[R_kst[l]])
            S.op("pool", nc.gpsimd.tensor_copy, out=kst[64:128, l, :, :], in_=khi[64:128, :, NB * 128:(NB + 1) * 128], reads=[R_kTd[NB]], writes=[R_kst[l]])
            S.op("pool", nc.gpsimd.tensor_copy, out=vst[:, l, :], in_=vtm[:, NB, :], reads=[R_vtm[NB]], writes=[R_vst[l]])

            S.lab = 'pool'
            for c in range(8):
                wi = c // 2
                bi = nbank()
                for b in range(NB):
                    first = gblock0 and b == 0
                    S.op("pe", nc.tensor.matmul, banks[bi][:, b * 128:(b + 1) * 128], ptm[:, b + 1, c * 128:(c + 1) * 128], poolmat(wi, 2 if first else 0), start=True, stop=first,
                         reads=[R_ptm[b + 1], R_const], writes=[R_bank[bi]])
                    if not first:
                        S.op("pe", nc.tensor.matmul, banks[bi][:, b * 128:(b + 1) * 128], ptm[:, b, c * 128:(c + 1) * 128], poolmat(wi, 1), start=False, stop=True,
                             reads=[R_ptm[b], R_const], writes=[R_bank[bi]])
                S.op("act", nc.scalar.copy, out=pooledT[:, c, :], in_=banks[bi][:], reads=[R_bank[bi]], writes=[R_qtm])
            S.op("pool", nc.gpsimd.tensor_copy, out=pst[:, l, :], in_=ptm[:, NB, :], reads=[R_ptm[NB]], writes=[R_pst[l]])
            for j in range(8):
                gi, jj = j // 2, j % 2
                bi = nbank()
                for k in range(2):
                    S.op("pe", nc.tensor.matmul, banks[bi][:], wpg[:, gi, k, jj * 128:(jj + 1) * 128], pooledT[:, 2 * gi + k, :], start=(k == 0), stop=(k == 1),
                         reads=[R_wsm[ws], R_qtm], writes=[R_bank[bi]])
                S.op("dve", nc.vector.tensor_scalar, out=mixedT[:, j, :], in0=banks[bi][:], scalar1=vec[:, l, 12, j:j + 1], scalar2=None, op0=ALU.mult,
                     reads=[R_bank[bi], R_const], writes=[R_mixedT])

            S.lab = 'merge'
            R_qT_all = [R_qT[g][b] for g in range(4) for b in range(NB)]
            for j in range(8):
                gsl, Rg = load_unit_raw(wg_b[l, j], l)
                gv = [gsl.rearrange("p (k b c) -> p k b c", k=8, b=3)[:, :, bi_, :] for bi_ in range(3)]
                gb = []
                for bi_ in range(3):
                    bk = fm_group(gv[bi_], Rg, 0, uT, R_uT)
                    gb.append(bk)
                    S.op("act", nc.scalar.activation, out=th[:, bi_, :], in_=banks[bk][:], func=AF.Tanh, scale=0.5, reads=[R_bank[bk]], writes=[R_th[bi_]])
                bsl, Rb = load_unit_raw(wbr_b[l, j], l)
                bv = [bsl.rearrange("p (k b c) -> p k b c", k=8, b=3)[:, :, bi_, :] for bi_ in range(3)]
                for bi_, (src, Rsrc) in enumerate(((qT, R_qT_all), (rnnT, R_rnnT), (mixedT, R_mixedT))):
                    bk = fm_group(bv[bi_], Rb, 0, src, Rsrc)
                    S.op("dve", nc.vector.scalar_tensor_tensor, out=th[:, bi_, :], in0=th[:, bi_, :], scalar=1.0, in1=banks[bk][:], op0=ALU.add, op1=ALU.mult,
                         reads=[R_th[bi_], R_bank[bk]], writes=[R_th[bi_]])
                S.op("pool", nc.gpsimd.tensor_tensor, out=th[:, 0, :], in0=th[:, 0, :], in1=th[:, 1, :], op=ALU.add, reads=[R_th[0], R_th[1]], writes=[R_th[0]])
                S.op("pool", nc.gpsimd.tensor_tensor, out=merged[:, j, :], in0=th[:, 0, :], in1=th[:, 2, :], op=ALU.add, reads=[R_th[0], R_th[2]], writes=[R_qtm])

            S.lab = 'wout'
            wo = {}

            def wout_src(j):
                if j % 4 == 0:
                    wo["v"], wo["R"] = load_unit([w_out_b[l][:, (j // 4) * 512:(j // 4 + 1) * 512]], l)
                return fm_group(wo["v"][0], wo["R"], (j % 4) * 128, merged, R_qtm)
            postnorm(l, 1, wout_src, 0.5)

            S.lab = 'mlp_pre'
            prenorm(l, 2)
            actT = [qT[:, i, :] for i in range(8)] + [rnnT[:, i, :] for i in range(8)] + [mixedT[:, i, :] for i in range(8)] + [pooledT[:, i, :] for i in range(8)]
            R_act = [R_qT_all] * 8 + [[R_rnnT]] * 8 + [[R_mixedT]] * 8 + [[R_qtm]] * 8
            S.lab = 'up'
            for f in range(32):
                if f % 4 == 0:
                    (uv,), Ru = load_unit([w_up_b[l][:, (f // 4) * 512:(f // 4 + 1) * 512]], l)
                bk = fm_group(uv, Ru, (f % 4) * 128, uT, R_uT)
                rp = f % 2
                S.op("act", nc.scalar.activation, out=rl[:, rp, :], in_=banks[bk][:], func=AF.Relu, reads=[R_bank[bk]], writes=[R_rl[rp]])
                if f % 2 == 0:
                    S.op("dve", nc.vector.tensor_tensor, out=actT[f], in0=rl[:, rp, :], in1=rl[:, rp, :], op=ALU.mult, reads=[R_rl[rp]], writes=R_act[f])
                else:
                    S.op("pool", nc.gpsimd.tensor_tensor, out=actT[f], in0=rl[:, rp, :], in1=rl[:, rp, :], op=ALU.mult, reads=[R_rl[rp]], writes=R_act[f])
            S.lab = 'down'
            dn = {}

            def down_src(j):
                jp, jj = j // 2, j % 2
                if jj == 0:
                    dn["b"] = [nbank(), nbank()]
                    for half in range(2):
                        (dv,), Rd = load_unit([w_down_b[l][half * 2048:(half + 1) * 2048, jp * 256:(jp + 1) * 256]], l)
                        for k in range(16):
                            for j2 in range(2):
                                kk = half * 16 + k
                                S.op("pe", nc.tensor.matmul, banks[dn["b"][j2]][:], dv[:, k, j2 * 128:(j2 + 1) * 128], actT[kk], start=(kk == 0), stop=(kk == 31),
                                     reads=[Rd] + R_act[kk], writes=[R_bank[dn["b"][j2]]])
                return dn["b"][jj]
            postnorm_down(l, down_src)

        def postnorm_down(l, down_src):
            postnorm(l, 3, down_src, 1.0)

        for t in range(NT):
            S.lab = 'xin'
            S.op("sp", lambda t=t: [nc.sync.dma_start(out=xst[:], in_=x_d[t * TT:(t + 1) * TT, :].rearrange("(b p) d -> p b d", p=128)),
                                    nc.sync.dma_start(out=ropet[:], in_=rope_d[t * TT:(t + 1) * TT, :].rearrange("(b p) n -> p b n", p=128))],
                 writes=R_xs + [R_ropet], dma=True, chan="xin")
            for c in range(8):
                bi = nbank()
                for b in range(NB):
                    S.op("pe", nc.tensor.transpose, banks[bi][:, b * 128:(b + 1) * 128], xst[:, b, c * 128:(c + 1) * 128], identf[:],
                         reads=[R_xs[b * 2 + c // 4], R_const], writes=[R_bank[bi]])
                if c % 2 == 0:
                    S.op("dve", nc.vector.tensor_copy, out=h[:, c, :], in_=banks[bi][:], reads=[R_bank[bi]], writes=[R_h[c]])
                else:
                    S.op("act", nc.scalar.copy, out=h[:, c, :], in_=banks[bi][:], reads=[R_bank[bi]], writes=[R_h[c]])
            for l in range(L):
                layer(t, l)
            S.lab = 'xout'
            for b in range(NB):
                for half in range(2):
                    bi = nbank()
                    for cc_ in range(4):
                        c = half * 4 + cc_
                        S.op("pe", nc.tensor.transpose, banks[bi][:, cc_ * 128:(cc_ + 1) * 128], h[:, c, b * 128:(b + 1) * 128], identf[:],
                             reads=[R_h[c], R_const], writes=[R_bank[bi]])
                    if half == 0:
                        S.op("dve", nc.vector.tensor_copy, out=xst[:, b, half * 512:(half + 1) * 512], in_=banks[bi][:], reads=[R_bank[bi]], writes=[R_xs[b * 2 + half]])
                    else:
                        S.op("act", nc.scalar.copy, out=xst[:, b, half * 512:(half + 1) * 512], in_=banks[bi][:], reads=[R_bank[bi]], writes=[R_xs[b * 2 + half]])
            S.op("sp", lambda t=t: [nc.sync.dma_start(out=out_d[t * TT:(t + 1) * TT, :].rearrange("(b p) d -> p b d", p=128), in_=xst[:])],
                 reads=R_xs, dma=True, chan="xout")
        fin = Res("fin")
        S.op("sp", nc.sync.nop, reads=R_xs, writes=[fin] + R_xs)
        import os
        if os.environ.get('KLABELS'):
            open(os.environ['KLABELS'], 'w').write('\n'.join(o.lab for o in S.ops if o.eng == 'pe' and not o.isdma and o.lab != 'mask'))
        stats = S.emit(es)
    return nc, stats


_VEC_ORDER = ["norm_mix_pre", "norm_mix_post", "norm_mlp_pre", "norm_mlp_post", None, None, None, None,
              "conv_b", "b_rg_a", "b_rg_i", "lru_lambda", "pool_scale"]


def _pack_layer_inputs(inp, layers):
    L = len(layers)
    vecs = np.zeros((L, 128, NV, 8), np.float32)
    for li, l in enumerate(layers):
        for vi, name in enumerate(_VEC_ORDER):
            if name is None:
                v = np.asarray(inp["conv_w"])[l, vi - 4]
            else:
                v = np.asarray(inp[name])[l]
            vecs[li, :, vi, :] = v.reshape(8, 128).T
    sinks = np.zeros((128, L, 8), np.float32)
    for li, l in enumerate(layers):
        s = np.asarray(inp["attn_sinks"])[l]
        sinks[0:64, li, :] = s[0::2][None, :]
        sinks[64:128, li, :] = s[1::2][None, :]
    m = {"vecs": vecs.reshape(L, 128, NV * 8), "sinks": sinks.reshape(128, L * 8)}
    for name in ["w_in", "w_attn_br", "w_rnn_br", "w_pool_br", "w_out", "w_mlp_up", "w_mlp_down", "w_rg_a", "w_rg_i", "w_pool_groups"]:
        m[name] = np.ascontiguousarray(np.asarray(inp[name])[layers], dtype=np.float32)
    return m


_CACHE = {}


def _get_prog(L, NT):
    key = (L, NT)
    if key not in _CACHE:
        _CACHE[key] = build(L, NT)[0]
    return _CACHE[key]


def run_layers(x_all, inp, layers, ncores=NCORE, NT=SEQ // TT):
    identf, cbf, rope = _const_tables()
    nc = _get_prog(len(layers), NT)
    base = _pack_layer_inputs(inp, layers)
    base["identf"] = identf
    base["cbf"] = cbf
    base["rope"] = np.ascontiguousarray(rope[:NT * TT])
    in_maps = []
    for c in range(ncores):
        m = dict(base)
        m["x"] = np.ascontiguousarray(x_all[c], dtype=np.float32)
        in_maps.append(m)
    res = run_bass_kernel_spmd(nc, in_maps, core_ids=list(range(ncores)))
    return np.stack([np.asarray(r["out"]) for r in res.results], axis=0)


def kernel(**inputs):
    x = np.asarray(inputs["x"], dtype=np.float32)
    if FUSED:
        out = run_layers(x, inputs, list(range(DEPTH)))
    else:
        out = x
        for l in range(DEPTH):
            out = run_layers(out, inputs, [l])
    return out.astype(np.float32)
```
